# Optimizing a Trainium2 kernel written in Bass

```python
import jax, jax.numpy as jnp
from jax import lax
import numpy as np

D_MODEL = 1024
BATCH = 2
SEQ = 8192
DEPTH = 1

CONV_WIDTH = D_MODEL // 2
RWKV_WIDTH = D_MODEL - CONV_WIDTH
RWKV_HEAD = 64
RWKV_HEADS = RWKV_WIDTH // RWKV_HEAD
CONV_K = 31
DECAY_LORA = 64
A_LORA = 64
GATE_LORA = 160
D_FF = 2816
FFN_CONV_K = 3
N_RWKV_IN = 3 * RWKV_WIDTH + DECAY_LORA + A_LORA + GATE_LORA
N_IN = 2 * CONV_WIDTH + N_RWKV_IN
RWKV_SPLITS = (RWKV_WIDTH, 2 * RWKV_WIDTH, 3 * RWKV_WIDTH,
               3 * RWKV_WIDTH + DECAY_LORA, 3 * RWKV_WIDTH + DECAY_LORA + A_LORA)
RMS_EPS = 1e-6
LN_EPS = 1e-5
LNX_EPS = 64e-5

kernel_name = 'hymba_conformer_rwkv7_convffn_adaln'


def rms_norm(x, g):
    xf = x.astype(jnp.float32)
    y = xf * lax.rsqrt(jnp.mean(xf * xf, axis=-1, keepdims=True) + RMS_EPS)
    return (y * g.astype(jnp.float32)).astype(x.dtype)


def layer_norm(x, eps):
    xf = x.astype(jnp.float32)
    mu = jnp.mean(xf, axis=-1, keepdims=True)
    d = xf - mu
    return d * lax.rsqrt(jnp.mean(d * d, axis=-1, keepdims=True) + eps)


def causal_dwconv(x, w, b):
    K, C = w.shape
    y = lax.conv_general_dilated(x, w[:, None, :].astype(x.dtype), window_strides=(1,),
                                 padding=[(K - 1, 0)],
                                 dimension_numbers=('NWC', 'WIO', 'NWC'),
                                 feature_group_count=C)
    return y + b


def wkv7_scan(r, decay, k, v, kk, a):
    B, T, H, N = r.shape

    def step(S, inp):
        r_t, w_t, k_t, v_t, kk_t, a_t = inp
        s_kk = jnp.einsum('bhij,bhj->bhi', S, kk_t)
        S = (S * w_t[:, :, None, :] - s_kk[..., None] * (a_t * kk_t)[:, :, None, :]
             + v_t[..., None] * k_t[:, :, None, :])
        return S, jnp.einsum('bhij,bhj->bhi', S, r_t)

    xs = tuple(jnp.moveaxis(z.astype(jnp.float32), 1, 0) for z in (r, decay, k, v, kk, a))
    S0 = jnp.zeros((B, H, N, N), jnp.float32)
    _, ys = lax.scan(step, S0, xs)
    return jnp.moveaxis(ys, 0, 1)


def token_mix(h, w_in, conv_dw_w, conv_dw_b, conv_ln_g, conv_ln_b, rwkv_mu, w0, w2,
              a0, a2, g2, k_k, k_a, r_k, lnx_g, lnx_b, w_out):
    B, T, _ = h.shape
    f32 = jnp.float32
    p = h @ w_in
    p_conv, p_rw = p[..., :2 * CONV_WIDTH], p[..., 2 * CONV_WIDTH:]

    u, u_gate = jnp.split(p_conv, 2, axis=-1)
    u = u * jax.nn.sigmoid(u_gate)
    u = causal_dwconv(u, conv_dw_w, conv_dw_b)
    u = jax.nn.silu((layer_norm(u, LN_EPS) * conv_ln_g + conv_ln_b).astype(h.dtype))

    prev = jnp.pad(p_rw, ((0, 0), (1, 0), (0, 0)))[:, :-1]
    p_rw = p_rw + (prev - p_rw) * rwkv_mu
    r, k, v, wd, ad, gd = jnp.split(p_rw, RWKV_SPLITS, axis=-1)
    w_log = -jax.nn.softplus(-(w0 + jnp.tanh(wd) @ w2).astype(f32)) - 0.5
    decay = jnp.exp(-jnp.exp(w_log))
    a = jax.nn.sigmoid(a0 + ad @ a2)
    g = jax.nn.sigmoid(gd) @ g2
    heads = lambda z: z.reshape(B, T, RWKV_HEADS, RWKV_HEAD)
    kk = heads(k * k_k).astype(f32)
    kk = kk / jnp.maximum(jnp.sqrt(jnp.sum(kk * kk, axis=-1, keepdims=True)), 1e-12)
    k = k * (1.0 + (a - 1.0) * k_a)
    rh, kh, vh = heads(r), heads(k), heads(v)
    y = wkv7_scan(rh, heads(decay), kh, vh, kk, heads(a))
    y = layer_norm(y, LNX_EPS).reshape(B, T, RWKV_WIDTH) * lnx_g + lnx_b
    bonus = jnp.sum((rh * kh * r_k).astype(f32), axis=-1, keepdims=True) * vh.astype(f32)
    y = ((y + bonus.reshape(B, T, RWKV_WIDTH)) * g).astype(h.dtype)

    return jnp.concatenate([u, y], axis=-1) @ w_out


def channel_mix(h, w_up, ffn_dw_w, ffn_dw_b, w_down):
    z = causal_dwconv(h @ w_up, ffn_dw_w, ffn_dw_b)
    z_gate, z_val = jnp.split(z, 2, axis=-1)
    return (jax.nn.silu(z_gate) * z_val) @ w_down


def setup_inputs(seed: int = 0) -> dict:
    key = jax.random.key(seed)
    ks = jax.random.split(key, 32)
    L = DEPTH
    nrm = lambda k, shape, s: jax.random.normal(k, shape, jnp.float32) * s
    return {
        'x': nrm(ks[0], (BATCH, SEQ, D_MODEL), 1.0),
        'c': nrm(ks[1], (BATCH, D_MODEL), 1.0),
        'ada_w': nrm(ks[2], (L, D_MODEL, 6 * D_MODEL), 0.5 * D_MODEL ** -0.5),
        'ada_b': nrm(ks[3], (L, 6 * D_MODEL), 0.02),
        'mix_pre_g': 1.0 + nrm(ks[4], (L, D_MODEL), 0.05),
        'mix_post_g': 1.0 + nrm(ks[5], (L, D_MODEL), 0.05),
        'w_in': nrm(ks[6], (L, D_MODEL, N_IN), D_MODEL ** -0.5),
        'conv_dw_w': nrm(ks[7], (L, CONV_K, CONV_WIDTH), CONV_K ** -0.5),
        'conv_dw_b': nrm(ks[8], (L, CONV_WIDTH), 0.02),
        'conv_ln_g': 1.0 + nrm(ks[9], (L, CONV_WIDTH), 0.05),
        'conv_ln_b': nrm(ks[10], (L, CONV_WIDTH), 0.02),
        'rwkv_mu': jax.random.uniform(ks[11], (L, N_RWKV_IN), jnp.float32),
        'w0': jax.random.uniform(ks[12], (L, RWKV_WIDTH), jnp.float32, minval=-5.0, maxval=1.0),
        'w2': nrm(ks[13], (L, DECAY_LORA, RWKV_WIDTH), 0.5 * DECAY_LORA ** -0.5),
        'a0': nrm(ks[14], (L, RWKV_WIDTH), 0.5),
        'a2': nrm(ks[15], (L, A_LORA, RWKV_WIDTH), 0.5 * A_LORA ** -0.5),
        'g2': nrm(ks[16], (L, GATE_LORA, RWKV_WIDTH), GATE_LORA ** -0.5),
        'k_k': 0.85 + nrm(ks[17], (L, RWKV_WIDTH), 0.05),
        'k_a': 1.0 + nrm(ks[18], (L, RWKV_WIDTH), 0.05),
        'r_k': nrm(ks[19], (L, RWKV_HEADS, RWKV_HEAD), 0.1),
        'lnx_g': 1.0 + nrm(ks[20], (L, RWKV_WIDTH), 0.05),
        'lnx_b': nrm(ks[21], (L, RWKV_WIDTH), 0.02),
        'w_out': nrm(ks[22], (L, D_MODEL, D_MODEL), D_MODEL ** -0.5),
        'ffn_pre_g': 1.0 + nrm(ks[23], (L, D_MODEL), 0.05),
        'ffn_post_g': 1.0 + nrm(ks[24], (L, D_MODEL), 0.05),
        'w_up': nrm(ks[25], (L, D_MODEL, 2 * D_FF), D_MODEL ** -0.5),
        'ffn_dw_w': nrm(ks[26], (L, FFN_CONV_K, 2 * D_FF), FFN_CONV_K ** -0.5),
        'ffn_dw_b': nrm(ks[27], (L, 2 * D_FF), 0.02),
        'w_down': nrm(ks[28], (L, D_FF, D_MODEL), D_FF ** -0.5),
    }


def reference(x, c, ada_w, ada_b, mix_pre_g, mix_post_g, w_in, conv_dw_w, conv_dw_b,
              conv_ln_g, conv_ln_b, rwkv_mu, w0, w2, a0, a2, g2, k_k, k_a, r_k,
              lnx_g, lnx_b, w_out, ffn_pre_g, ffn_post_g, w_up, ffn_dw_w, ffn_dw_b,
              w_down):
    for l in range(DEPTH):
        mod = jax.nn.silu(c) @ ada_w[l] + ada_b[l]
        sh_m, sc_m, gt_m, sh_f, sc_f, gt_f = jnp.split(mod[:, None, :], 6, axis=-1)
        h = rms_norm(x, mix_pre_g[l]) * (1.0 + sc_m) + sh_m
        m = token_mix(h, w_in[l], conv_dw_w[l], conv_dw_b[l], conv_ln_g[l], conv_ln_b[l],
                      rwkv_mu[l], w0[l], w2[l], a0[l], a2[l], g2[l], k_k[l], k_a[l],
                      r_k[l], lnx_g[l], lnx_b[l], w_out[l])
        x = x + gt_m * rms_norm(m, mix_post_g[l])
        h = rms_norm(x, ffn_pre_g[l]) * (1.0 + sc_f) + sh_f
        f = channel_mix(h, w_up[l], ffn_dw_w[l], ffn_dw_b[l], w_down[l])
        x = x + gt_f * rms_norm(f, ffn_post_g[l])
    return x
```

```python
import contextlib
import numpy as np
import concourse.bass as bass
import concourse.mybir as mybir
from concourse.bass_utils import run_bass_kernel_spmd

F32 = mybir.dt.float32
BF16 = mybir.dt.bfloat16
ALU = mybir.AluOpType
AF = mybir.ActivationFunctionType

D = 1024
SEQ = 8192
NCORE = 8
TOK = 2048
HALO = 128
NT = TOK + HALO
NX = NT + 1
NRW = 1824
DFF = 2816
RMS_EPS = 1e-6
LN_EPS = 1e-5
LNX_EPS = 64e-5
DECAY_C = float(np.exp(-0.5))


class Buf:
    __slots__ = ("name", "lw", "rd", "excl")

    def __init__(self, name):
        self.name = name
        self.lw = None
        self.rd = []
        self.excl = False


class Sched:
    ENG = ("pe", "act", "dve", "pool", "sp")

    def __init__(self, nc):
        self.nc = nc
        self.ops = {e: [] for e in self.ENG}
        self.cnt = {e: 0 for e in self.ENG}
        self.pos = {e: 0 for e in self.ENG}
        self.seen = {e: {} for e in self.ENG}
        self.pending = {e: [] for e in self.ENG}
        self.sems = {}
        self.semval = {}
        self.epoch = {e: 0 for e in self.ENG}
        for e in self.ENG:
            self.sems[e + "#0"] = nc.alloc_semaphore(name="s_" + e + "_0")
        self.EPOCH_MAX = 8000
        self.nsem = 0

    def new_sem(self, name):
        k = "d_" + name
        self.sems[k] = self.nc.alloc_semaphore(name=k)
        self.semval[k] = 0
        return k

    def _need(self, eng, ev, waits, same_ok_dist=None):
        if ev is None:
            return
        semkey, val, weng, wpos = ev
        if weng == eng and semkey.startswith(eng + "#"):
            if eng == "pe":
                return
            if same_ok_dist is None:
                return
            if self.pos[eng] - wpos > same_ok_dist:
                return
        if self.seen[eng].get(semkey, 0) >= val:
            return
        waits[semkey] = max(waits.get(semkey, 0), val)

    def op(self, eng, fn, reads=(), writes=(), signal=True):
        waits = {}
        for b in reads:
            self._need(eng, b.lw, waits, same_ok_dist=8)
            if b.excl:
                for ev in b.rd:
                    self._need(eng, ev, waits, same_ok_dist=None)
        for b in writes:
            self._need(eng, b.lw, waits, same_ok_dist=8)
            for ev in b.rd:
                self._need(eng, ev, waits, same_ok_dist=None)
        for k, v in waits.items():
            self.seen[eng][k] = v
        wl = list(waits.items())
        if not signal:
            self.ops[eng].append((wl, fn, None, 0))
            self.pos[eng] += 1
            self.pending[eng].append((list(reads), list(writes)))
            return None
        if self.cnt[eng] >= self.EPOCH_MAX:
            self.epoch[eng] += 1
            self.cnt[eng] = 0
            nk = eng + "#%d" % self.epoch[eng]
            self.sems[nk] = self.nc.alloc_semaphore(name="s_%s_%d" % (eng, self.epoch[eng]))
        sk = eng + "#%d" % self.epoch[eng]
        self.cnt[eng] += 1
        ev = (sk, self.cnt[eng], eng, self.pos[eng])
        self.ops[eng].append((wl, fn, sk, 1))
        self.pos[eng] += 1
        groups = self.pending[eng] + [(list(reads), list(writes))]
        self.pending[eng] = []
        for rds, wrs in groups:
            for b in rds:
                b.rd.append(ev)
                if len(b.rd) > 32:
                    b.rd = b.rd[-32:]
            for b in wrs:
                b.lw = ev
                b.rd = []
        return ev

    def dma(self, queue, semkey, fn, reads=(), writes=()):
        waits = {}
        big = 1 << 30
        for b in reads:
            self._need(queue, b.lw, waits, same_ok_dist=big)
        for b in writes:
            self._need(queue, b.lw, waits, same_ok_dist=big)
            for ev in b.rd:
                self._need(queue, ev, waits, same_ok_dist=big)
        for k, v in waits.items():
            self.seen[queue][k] = v
        self.semval[semkey] += 16
        ev = (semkey, self.semval[semkey], "dma", 0)
        self.ops[queue].append((list(waits.items()), fn, semkey, 16))
        self.pos[queue] += 1
        for b in reads:
            b.rd.append(ev)
        for b in writes:
            b.lw = ev
            b.rd = []
        return ev

    def raw(self, eng, waits, fn, semkey, inc):
        self.ops[eng].append((list(waits), fn, semkey, inc))
        self.pos[eng] += 1

    def wait_all(self, eng, bufs):
        waits = {}
        big = 1 << 30
        for b in bufs:
            self._need(eng, b.lw, waits, same_ok_dist=big)
            for ev in b.rd:
                self._need(eng, ev, waits, same_ok_dist=big)
        for k, v in waits.items():
            self.seen[eng][k] = v
        self.ops[eng].append((list(waits.items()), None, None, 0))

    def emit(self):
        nc = self.nc
        sems = self.sems
        for e in self.ENG:
            assert not self.pending[e], "unsignalled tail on " + e

        def run(engobj, lst):
            for wl, fn, sk, inc in lst:
                for k, v in wl:
                    engobj.wait_ge(sems[k], v)
                if fn is None:
                    continue
                ins = fn(engobj)
                if sk is not None:
                    ins.then_inc(sems[sk], inc)

        with nc.Block() as block:
            @block.tensor
            def _(e):
                run(e, self.ops["pe"])

            @block.scalar
            def _(e):
                run(e, self.ops["act"])

            @block.vector
            def _(e):
                run(e, self.ops["dve"])

            @block.gpsimd
            def _(e):
                run(e, self.ops["pool"])

            @block.sync
            def _(e):
                run(e, self.ops["sp"])
        self.ops = {e: [] for e in self.ENG}


class StopBuild(Exception):
    pass


class T:
    def __init__(self, ap_tensor, name):
        self.t = ap_tensor
        self.b = Buf(name)
        self.sem = None

    def __getitem__(self, idx):
        return self.t[idx]


class Prog:
    def __init__(self, debug=None, phases=("0", "A", "X", "B", "C")):
        self.debug = debug or {}
        self.phases = phases
        self.nc = bass.Bass("TRN2", target_bir_lowering=False)
        self.S = Sched(self.nc)
        self.dram = {}
        self.dbuf = {}
        self.nbank = 0

    def din(self, name, shape, dt=F32):
        t = self.nc.dram_tensor(name, list(shape), dt, kind="ExternalInput")
        self.dram[name] = t
        self.dbuf[name] = Buf(name)
        return t.ap()

    def dout(self, name, shape, dt=F32):
        t = self.nc.dram_tensor(name, list(shape), dt, kind="ExternalOutput")
        self.dram[name] = t
        self.dbuf[name] = Buf(name)
        return t.ap()

    def dint(self, name, shape, dt=F32):
        t = self.nc.dram_tensor(name, list(shape), dt)
        self.dram[name] = t
        self.dbuf[name] = Buf(name)
        return t.ap()

    def sb(self, stack, name, shape, dt=F32):
        t = stack.enter_context(self.nc.sbuf_tensor("sb_" + name, list(shape), dt))
        return T(t, name)

    def ps(self, stack, name, shape=(128, 512), dt=F32):
        t = stack.enter_context(self.nc.psum_tensor("ps_" + name, list(shape), dt))
        return T(t, name)

    def mm(self, bank, out, lhsT, rhs, start, stop, rd, signal=None):
        sig = stop if signal is None else signal
        self.S.op("pe", lambda e, o=out, l=lhsT, r=rhs, st=start, sp=stop: e.matmul(
            o, l, r, start=st, stop=sp, skip_group_check=True),
            reads=[x.b for x in rd], writes=[bank.b], signal=sig)

    def act(self, out, in_, func, wr, rd, bias=None, scale=None):
        kw = {}
        if func == AF.Copy and scale is not None and not isinstance(scale, float):
            func = AF.Identity
        if bias is not None:
            kw["bias"] = bias
        if scale is not None:
            kw["scale"] = scale
        self.S.op("act", lambda e, o=out, i=in_, f=func, kw=kw: e.activation(out=o, in_=i, func=f, **kw),
                  reads=[x.b for x in rd], writes=[x.b for x in wr])

    def tt(self, eng, out, in0, in1, op, wr, rd):
        self.S.op(eng, lambda e, o=out, a=in0, b=in1, p=op: e.tensor_tensor(out=o, in0=a, in1=b, op=p),
                  reads=[x.b for x in rd], writes=[x.b for x in wr])

    def ts(self, eng, out, in0, s1, s2, op0, op1, wr, rd):
        if op1 is None:
            self.S.op(eng, lambda e, o=out, a=in0, s=s1, p=op0: e.tensor_scalar(
                out=o, in0=a, scalar1=s, scalar2=None, op0=p),
                reads=[x.b for x in rd], writes=[x.b for x in wr])
        else:
            self.S.op(eng, lambda e, o=out, a=in0, s=s1, s_=s2, p=op0, p_=op1: e.tensor_scalar(
                out=o, in0=a, scalar1=s, scalar2=s_, op0=p, op1=p_),
                reads=[x.b for x in rd], writes=[x.b for x in wr])

    def stt(self, out, in0, scalar, in1, op0, op1, wr, rd):
        self.S.op("dve", lambda e, o=out, a=in0, s=scalar, b=in1, p=op0, p_=op1: e.scalar_tensor_tensor(
            out=o, in0=a, scalar=s, in1=b, op0=p, op1=p_),
            reads=[x.b for x in rd], writes=[x.b for x in wr])

    def rsqrt(self, out, in_, eps, scale, wr_tile, rd):
        self.act(out, in_, AF.Ln, [wr_tile], rd, bias=float(eps), scale=float(scale))
        self.act(out, out, AF.Exp, [wr_tile], [wr_tile], scale=-0.5)

    def copy(self, eng, out, in_, wr, rd):
        if eng == "act":
            self.act(out, in_, AF.Copy, wr, rd)
        else:
            self.S.op(eng, lambda e, o=out, i=in_: e.tensor_copy(out=o, in_=i),
                      reads=[x.b for x in rd], writes=[x.b for x in wr])

    def memset(self, eng, ap, val, wr):
        self.S.op(eng, lambda e, a=ap, v=val: e.memset(a, v), reads=[], writes=[x.b for x in wr])

    def dma(self, queue, sem, out, in_, wr=(), rd=(), dwr=(), drd=()):
        owner = (list(wr) + list(rd))[0]
        if getattr(owner, "sem", None) is None:
            owner.sem = self.S.new_sem("t%d" % len(self.S.sems))
        sem = owner.sem
        self.S.dma(queue, sem, lambda e, o=out, i=in_: e.dma_start(out=o, in_=i),
                   reads=[x.b for x in rd] + [self.dbuf[n] for n in drd],
                   writes=[x.b for x in wr] + [self.dbuf[n] for n in dwr])

    def stop(self, level):
        if self.debug.get("stop") == level:
            raise StopBuild()

    def dbg(self, name, tile_ap, shape, rd, dt=F32):
        if name not in self.debug:
            return
        o = self.dout("dbg_" + name, shape, dt)
        self.dma("sp", None, o, tile_ap, rd=rd, dwr=["dbg_" + name])
        self.S.wait_all("sp", [self.dbuf["dbg_" + name]])


def build_program(debug=None, phases=("0", "A", "X", "B", "C")):
    P = Prog(debug, phases)
    nc, S = P.nc, P.S

    xT = P.din("xT", [D, NX])
    cvec = P.din("cvec", [128, 8])
    ada_w = P.din("ada_w", [D, 1536])
    ada_b = P.din("ada_b", [128, 12])
    gvec = P.din("gvec", [128, 32])
    w_in = P.din("w_in", [D, 2848])
    cw = P.din("conv_w", [128, 4, 31])
    cvecs = P.din("conv_v", [128, 12])
    mu = P.din("mu", [128, 15])
    rwv = P.din("rwv", [128, 28])
    w2a2 = P.din("w2a2", [128, 512])
    g2 = P.din("g2", [160, 512])
    w_out = P.din("w_out", [D, D])
    w_up = P.din("w_up", [D, 2 * DFF])
    fw = P.din("ffn_w", [128, 44, 3])
    fb = P.din("ffn_b", [128, 44])
    w_down = P.din("w_down", [DFF, D])
    hvd = P.din("hv", [128, 1])
    cmaskd = P.din("cmask", [128, 4])
    consts = P.din("consts", [128, 5, 128])
    outT = P.dout("outT", [D, TOK])

    cc2_in = P.dint("cc2_in", [128, 12])
    cc2_out = P.dint("cc2_out", [4 * 128, 12])
    yq_s = P.dint("yq_s", [4, 128, NT], BF16)
    rp_s = P.dint("rp_s", [4, 128, NT], BF16)
    g_s = P.dint("g_s", [4, 128, NT], BF16)
    bon_s = P.dint("bon_s", [4, 128, NT], BF16)
    cc_in = P.dint("cc_in", [128, 1024])
    cc_out = P.dint("cc_out", [4 * 128, 1024])
    xm_s = P.dint("xm_s", [D, TOK + 2])

    blocks = [(0, 128)] + [(128 + 512 * i, 512) for i in range(4)]

    with contextlib.ExitStack() as gs:
        cst = P.sb(gs, "cst", [128, 5, 128])
        identb = P.sb(gs, "identb", [128, 128], BF16)
        onesb = P.sb(gs, "onesb", [128, 128], BF16)
        blkb = P.sb(gs, "blkb", [128, 128], BF16)
        modv = P.sb(gs, "modv", [128, 48])
        gv = P.sb(gs, "gv", [128, 32])
        Am = P.sb(gs, "Am", [128, 8])
        Gm = P.sb(gs, "Gm", [128, 8])
        Af = P.sb(gs, "Af", [128, 8])
        Gf = P.sb(gs, "Gf", [128, 8])
        hv = P.sb(gs, "hv", [128, 1])
        cmask = P.sb(gs, "cmask", [128, 4])
        zst = P.sb(gs, "zst", [128, 4, 128], BF16)
        banks = [P.ps(gs, "bank%d" % i) for i in range(8)]
        for b_ in banks:
            b_.b.excl = True
        sem_c = sem_x = sem_w = sem_o = sem_o2 = None

        held = set()

        def nextbank(hold=False):
            while True:
                b = banks[P.nbank % 8]
                P.nbank += 1
                if id(b) not in held:
                    break
            if hold:
                held.add(id(b))
            return b

        def release(b):
            held.discard(id(b))

        with contextlib.ExitStack() as st:
            csb = P.sb(st, "csb", [128, 8])
            scv = P.sb(st, "scv", [128, 8])
            adab = P.sb(st, "adab", [128, 12])
            wbuf = [P.sb(st, "adaw%d" % i, [128, 1536]) for i in range(3)]
            modrow = P.sb(st, "modrow", [1, 1536])
            modq = P.sb(st, "modq", [128, 12])
            P.dma("sp", sem_c, cst[:], consts, wr=[cst])
            P.dma("sp", sem_c, csb[:], cvec, wr=[csb])
            P.dma("sp", sem_c, adab[:], ada_b, wr=[adab])
            P.dma("sp", sem_c, gv[:], gvec, wr=[gv])
            P.dma("sp", sem_c, hv[:], hvd, wr=[hv])
            P.dma("sp", sem_c, cmask[:], cmaskd, wr=[cmask])
            P.copy("dve", identb[:], cst[:, 0, :], [identb], [cst])
            P.copy("dve", blkb[:], cst[:, 4, :], [blkb], [cst])
            P.memset("dve", onesb[:], 1.0, [onesb])
            P.act(scv[:], csb[:], AF.Silu, [scv], [csb])
            bk = [nextbank() for _ in range(3)]
            for kt in range(8):
                wb = wbuf[kt % 3]
                P.dma("sp" if kt % 2 else "pool", sem_w, wb[:], ada_w[kt * 128:(kt + 1) * 128, :], wr=[wb])
                for j in range(3):
                    P.mm(bk[j], bk[j][0:1, :], scv[:, kt:kt + 1], wb[:, j * 512:(j + 1) * 512],
                         kt == 0, kt == 7, [scv, wb], signal=True if j == 2 else None)
            for j in range(3):
                P.copy("act" if j % 2 else "dve", modrow[0:1, j * 512:(j + 1) * 512], bk[j][0:1, :], [modrow], [bk[j]])
            one1 = P.sb(st, "one1", [1, 1])
            P.memset("dve", one1[:], 1.0, [one1])
            bkt = nextbank()
            for t in range(12):
                P.mm(bkt, bkt[:, t:t + 1], modrow[0:1, t * 128:(t + 1) * 128], one1[0:1, 0:1], t == 0, True, [modrow, one1],
                     signal=(t == 11))
            P.tt("dve", modq[:], bkt[:, 0:12], adab[:], ALU.add, [modq], [bkt, adab])
            P.dma("pool", None, cc2_in, modq[:], rd=[modq], dwr=["cc2_in"])
            S.sems["cc2"] = nc.alloc_semaphore(name="cc2_sem")
            ev_ = P.dbuf["cc2_in"].lw
            S.raw("pool", [(ev_[0], ev_[1])], lambda e: e.collective_compute(
                "AllGather", ALU.bypass, replica_groups=[[0, 1, 2, 3], [4, 5, 6, 7]],
                ins=[P.dram["cc2_in"].ap().opt()], outs=[P.dram["cc2_out"].ap().opt()]), "cc2", 1)
            P.dbuf["cc2_out"].lw = ("cc2", 1, "dma", 0)
            P.dma("pool", None, modv[:].rearrange("p (r j) -> p r j", j=12), cc2_out.rearrange("(r p) j -> p r j", p=128),
                  wr=[modv], drd=["cc2_out"])
            for (dst, gcol, sccol, gate) in ((Am, 0, 8, False), (Gm, 8, 16, True), (Af, 16, 32, False), (Gf, 24, 40, True)):
                if gate:
                    P.stt(dst[:], gv[:, gcol:gcol + 8], 32.0, modv[:, sccol:sccol + 8], ALU.mult, ALU.mult, [dst], [gv, modv])
                else:
                    P.ts("dve", dst[:], modv[:, sccol:sccol + 8], 1.0, None, ALU.add, None, [dst], [modv])
                    P.stt(dst[:], gv[:, gcol:gcol + 8], 32.0, dst[:], ALU.mult, ALU.mult, [dst], [gv, dst])
            P.dbg("modv", modv[:], [128, 48], [modv])
            P.dbg("Am", Am[:], [128, 8], [Am])
            S.emit()

        def load_x(xb, c0, ncol):
            P.dma("sp", sem_x, xb[:, :, 0:ncol], xT.rearrange("(kt p) t -> p kt t", p=128)[:, :, c0:c0 + ncol], wr=[xb])

        def load_x_and_norm(xb, sq, rstd, tmp, hb, c0, ncol, Avec, shcol, do_load=True):
            if do_load:
                load_x(xb, c0, ncol)
            bk = nextbank()
            for kt in range(8):
                s = sq[kt % 2]
                P.act(s[:, 0:ncol], xb[:, kt, 0:ncol], AF.Square, [s], [xb])
                P.mm(bk, bk[:, 0:ncol], onesb[:], s[:, 0:ncol], kt == 0, kt == 7, [onesb, s], signal=True)
            P.rsqrt(rstd[:, 0:ncol], bk[:, 0:ncol], float(D * RMS_EPS), 1.0, rstd, [bk])
            for kt in range(8):
                t = tmp[kt % 2]
                P.tt("dve", t[:, 0:ncol], xb[:, kt, 0:ncol], rstd[:, 0:ncol], ALU.mult, [t], [xb, rstd])
                P.act(hb[:, kt, 0:ncol], t[:, 0:ncol], AF.Identity, [hb], [t, Avec, modv],
                      bias=modv[:, shcol + kt:shcol + kt + 1], scale=Avec[:, kt:kt + 1])

        if "A" in phases:
            with contextlib.ExitStack() as st:
                wrw = P.sb(st, "wrw", [128, 8, NRW], BF16)
                lw = P.sb(st, "lw", [128, 512], BF16)
                g2a = P.sb(st, "g2a", [128, 512], BF16)
                g2b = P.sb(st, "g2b", [32, 512], BF16)
                muv = P.sb(st, "muv", [128, 15])
                rv = P.sb(st, "rv", [128, 28])
                omka = P.sb(st, "omka", [128, 4])
                rkb = P.sb(st, "rkb", [128, 4, 128], BF16)
                eab = P.sb(st, "eab", [128, 256], BF16)
                msk = P.sb(st, "msk", [128, 2, 512], BF16)
                ii2 = P.sb(st, "ii2", [128, 256], BF16)
                ii4 = P.sb(st, "ii4", [128, 512], BF16)
                rmask = P.sb(st, "rmask", [128, 512])
                xb = P.sb(st, "xb", [128, 8, 513])
                rstd = P.sb(st, "rstd", [128, 513])
                hbs = [P.sb(st, "hb%d" % i, [128, 8, 513], BF16) for i in range(2)]
                stage = [P.sb(st, "stage%d" % i, [128, 513]) for i in range(2)]
                carry = P.sb(st, "carry", [128, 15])
                dsh = [P.sb(st, "dsh%d" % i, [128, 512]) for i in range(1)]
                pl = P.sb(st, "pl", [128, 3, 512])
                pp = [P.sb(st, "pp%d" % i, [128, 3, 512]) for i in range(2)]
                twb = P.sb(st, "twb", [128, 512], BF16)
                sgd = P.sb(st, "sgd", [128, 2, 512], BF16)
                sc = [P.sb(st, "sc%d" % i, [128, 513]) for i in range(9)]
                tmp = [sc[0], sc[1]]
                scb = [P.sb(st, "scb%d" % i, [128, 513], BF16) for i in range(2)]
                sq = scb
                Wend = P.sb(st, "Wend", [128, 4, 8])
                RhA = P.sb(st, "RhA", [128, 4, 512], BF16)
                KKA = P.sb(st, "KKA", [128, 4, 512], BF16)
                BhA = P.sb(st, "BhA", [128, 4, 512], BF16)
                KhA = P.sb(st, "KhA", [128, 4, 512], BF16)
                VbA = P.sb(st, "VbA", [128, 4, 512], BF16)
                gblk = P.sb(st, "gblk", [128, 4, 512], BF16)
                bonblk = P.sb(st, "bonblk", [128, 4, 512], BF16)
                yqblk = P.sb(st, "yqblk", [128, 4, 512], BF16)
                rpblk = P.sb(st, "rpblk", [128, 4, 512], BF16)
                TM = [P.sb(st, "TM%d" % i, [128, 1024], BF16) for i in range(4)]
                AMh = [[P.sb(st, "AMh%d_%d" % (i, sd), [128, 512], BF16) for sd in range(2)] for i in range(4)]
                AK = [P.sb(st, "AK%d" % sd, [128, 4, 128], BF16) for sd in range(2)]
                AVs = P.sb(st, "AVs", [128, 4, 128], BF16)
                PQ = [[P.sb(st, "PQ%d_%d" % (l, i), [128, 512], BF16) for i in range(4)] for l in range(2)]
                GG = [[P.sb(st, "GG%d_%d" % (l, i), [128, 512], BF16) for i in range(2)] for l in range(2)]
                KU = [P.sb(st, "KU%d" % i, [128, 512], BF16) for i in range(4)]
                RT = P.sb(st, "RT", [128, 4, 128], BF16)
                MT = [P.sb(st, "MT%d" % i, [128, 4, 128], BF16) for i in range(2)]
                N0 = [P.sb(st, "N0%d" % i, [128, 4, 128]) for i in range(2)]
                ZQ = [P.sb(st, "ZQ%d" % i, [128, 4, 128], BF16) for i in range(2)]
                ZP = [P.sb(st, "ZP%d" % i, [128, 4, 128], BF16) for i in range(2)]

                xflat = xb.t[:].rearrange("p k c -> p (k c)")
                wstA = []
                for i in range(2):
                    t_ = T(xflat[:, i * NRW:(i + 1) * NRW], "wstA%d" % i)
                    wstA.append(t_)
                for kt in range(8):
                    stg_ = wstA[kt % 2]
                    P.dma("sp", sem_w, stg_[:], w_in[kt * 128:(kt + 1) * 128, 1024:2848], wr=[stg_])
                    P.copy("act" if kt % 2 else "dve", wrw[:, kt, :], stg_[:], [wrw], [stg_])
                xb.b.rd = list(wstA[0].b.rd) + list(wstA[1].b.rd)
                P.dma("pool", sem_w, lw[:], w2a2, wr=[lw])
                P.dma("pool", sem_w, g2a[:], g2[0:128, :], wr=[g2a])
                P.dma("pool", sem_w, g2b[:], g2[128:160, :], wr=[g2b])
                P.dma("sp", sem_c, muv[:], mu, wr=[muv])
                P.dma("sp", sem_c, rv[:], rwv, wr=[rv])
                P.ts("dve", omka[:], rv[:, 12:16], -1.0, 1.0, ALU.mult, ALU.add, [omka], [rv])
                for pr in range(4):
                    P.ts("dve", rkb[:, pr, :], cst[:, 4, :], rv[:, 24 + pr:25 + pr], None, ALU.mult, None, [rkb], [cst, rv])
                P.memset("dve", eab[:], 0.0, [eab])
                P.copy("dve", eab[0:64, 0:128], cst[0:64, 0, :], [eab], [cst])
                P.copy("dve", eab[64:128, 128:256], cst[64:128, 0, :], [eab], [cst])
                P.ts("dve", msk[:, 0, 0:128], cst[:, 1, :], -1.0, None, ALU.mult, None, [msk], [cst])
                P.ts("dve", msk[:, 0, 128:256], cst[:, 2, :], -1.0, None, ALU.mult, None, [msk], [cst])
                P.copy("dve", msk[:, 0, 256:384], cst[:, 2, :], [msk], [cst])
                P.copy("dve", msk[:, 0, 384:512], cst[:, 3, :], [msk], [cst])
                for j in range(4):
                    P.copy("dve", msk[:, 1, j * 128:(j + 1) * 128], cst[:, 3, :], [msk], [cst])
                    P.copy("dve", ii4[:, j * 128:(j + 1) * 128], cst[:, 0, :], [ii4], [cst])
                for j in range(2):
                    P.copy("dve", ii2[:, j * 128:(j + 1) * 128], cst[:, 0, :], [ii2], [cst])
                P.memset("dve", rmask[:], 1.0, [rmask])
                P.memset("dve", rmask[:].rearrange("p (c t) -> p c t", t=64)[:, :, 0:1], 0.0, [rmask])
                P.memset("dve", carry[:], 0.0, [carry])
                for i in range(4):
                    P.memset("pool", KU[i][:], 0.0, [KU[i]])
                for i in range(2):
                    P.memset("pool", ZQ[i][:], 0.0, [ZQ[i]])
                    P.memset("pool", ZP[i][:], 0.0, [ZP[i]])
                    P.memset("pool", MT[i][:], 0.0, [MT[i]])
                for pr in range(4):
                    P.copy("dve", ZP[0][:, pr, :], cst[:, 0, :], [ZP[0]], [cst])
                zcur = 0
                ngroup = 0

                try:
                  P.stop(1)
                  for bi, (t0, n) in enumerate(blocks):
                      ncol = n + 1 if bi == 0 else n
                      c0 = 0 if bi == 0 else t0 + 1
                      hb = hbs[bi % 2]
                      if bi == 0:
                          load_x_and_norm(xb, sq, rstd, tmp, hb, c0, ncol, Am, 0, do_load=True)
                          load_x(xb, blocks[1][0] + 1, blocks[1][1])

                      def proj_shift(j, dst, M=128):
                          bk = nextbank()
                          stg = stage[j % 2]
                          dd = dsh[0]
                          for kt in range(8):
                              P.mm(bk, bk[0:M, 0:ncol], wrw[:, kt, j * 128:j * 128 + M], hb[:, kt, 0:ncol],
                                   kt == 0, kt == 7, [wrw, hb])
                          if bi == 0:
                              P.act(stg[0:M, 0:ncol], bk[0:M, 0:ncol], AF.Copy, [stg], [bk, hv], scale=hv[0:M, 0:1])
                          else:
                              P.copy("pool", stg[0:M, 0:1], carry[0:M, j:j + 1], [stg], [carry])
                              P.act(stg[0:M, 1:n + 1], bk[0:M, 0:n], AF.Copy, [stg], [bk])
                          P.tt("dve", dd[0:M, 0:n], stg[0:M, 0:n], stg[0:M, 1:n + 1], ALU.subtract, [dd], [stg])
                          P.stt(dst, dd[0:M, 0:n], muv[0:M, j:j + 1], stg[0:M, 1:n + 1], ALU.mult, ALU.add, [dstT[0]], [dd, muv, stg])
                          P.copy("pool", carry[0:M, j:j + 1], stg[0:M, n:n + 1], [carry], [stg])

                      dstT = [pl]
                      proj_shift(12, pl[:, 0, 0:n])
                      proj_shift(13, pl[:, 1, 0:n])
                      proj_shift(14, pl[0:32, 2, 0:n], M=32)
                      P.act(twb[0:64, 0:n], pl[0:64, 0, 0:n], AF.Tanh, [twb], [pl])
                      P.copy("dve", twb[64:128, 0:n], pl[64:128, 0, 0:n], [twb], [pl])
                      P.act(sgd[:, 0, 0:n], pl[:, 1, 0:n], AF.Sigmoid, [sgd], [pl])
                      P.act(sgd[0:32, 1, 0:n], pl[0:32, 2, 0:n], AF.Sigmoid, [sgd], [pl])

                      def proj_pair(pq):
                          ppq = pp[pq % 2]
                          dstT[0] = ppq
                          proj_shift(pq, ppq[:, 0, 0:n])
                          proj_shift(4 + pq, ppq[:, 1, 0:n])
                          proj_shift(8 + pq, ppq[:, 2, 0:n])

                      proj_pair(0)
                      for pr in range(4):
                          ppt = pp[pr % 2]
                          if pr + 1 < 4:
                              proj_pair(pr + 1)
                          r_ = ppt[:, 0, 0:n]
                          k_ = ppt[:, 1, 0:n]
                          v_ = ppt[:, 2, 0:n]
                          cs_ = slice(pr * 128, (pr + 1) * 128)
                          bkw = nextbank()
                          P.mm(bkw, bkw[:, 0:n], lw[0:64, cs_], twb[0:64, 0:n], True, True, [lw, twb])
                          sg_ = sc[0]
                          P.act(sg_[:, 0:n], bkw[:, 0:n], AF.Sigmoid, [sg_], [bkw, rv], bias=rv[:, pr:pr + 1])
                          bka = nextbank()
                          P.mm(bka, bka[:, 0:n], lw[64:128, cs_], twb[64:128, 0:n], True, True, [lw, twb])
                          a_ = sc[1]
                          P.act(a_[:, 0:n], bka[:, 0:n], AF.Sigmoid, [a_], [bka, rv], bias=rv[:, 4 + pr:5 + pr])
                          bkg = nextbank()
                          P.mm(bkg, bkg[:, 0:n], g2a[:, cs_], sgd[:, 0, 0:n], True, False, [g2a, sgd])
                          P.mm(bkg, bkg[:, 0:n], g2b[:, cs_], sgd[0:32, 1, 0:n], False, True, [g2b, sgd])
                          P.copy("act", gblk[:, pr, 0:n], bkg[:, 0:n], [gblk], [bkg])
                          P.act(scb[0][:, 0:n], k_, AF.Square, [scb[0]], [ppt, rv], scale=rv[:, 8 + pr:9 + pr])
                          bkn = nextbank()
                          P.mm(bkn, bkn[:, 0:n], blkb[:], scb[0][:, 0:n], True, True, [blkb, scb[0]])
                          rn_ = sc[2]
                          P.rsqrt(rn_[:, 0:n], bkn[:, 0:n], 1e-18, 1.0, rn_, [bkn])
                          kk_ = sc[3]
                          P.stt(kk_[:, 0:n], k_, rv[:, 8 + pr:9 + pr], rn_[:, 0:n], ALU.mult, ALU.mult, [kk_], [ppt, rv, rn_])
                          t1_ = sc[4]
                          P.ts("dve", t1_[:, 0:n], a_[:, 0:n], rv[:, 12 + pr:13 + pr], omka[:, pr:pr + 1], ALU.mult, ALU.add,
                               [t1_], [a_, rv, omka])
                          k2_ = sc[5]
                          P.tt("dve", k2_[:, 0:n], k_, t1_[:, 0:n], ALU.mult, [k2_], [ppt, t1_])
                          b_ = sc[4]
                          P.tt("pool", b_[:, 0:n], a_[:, 0:n], kk_[:, 0:n], ALU.mult, [b_], [a_, kk_])
                          P.tt("dve", scb[1][:, 0:n], r_, k2_[:, 0:n], ALU.mult, [scb[1]], [ppt, k2_])
                          bkb = nextbank()
                          P.mm(bkb, bkb[:, 0:n], rkb[:, pr, :], scb[1][:, 0:n], True, True, [rkb, scb[1]])
                          P.tt("dve", bonblk[:, pr, 0:n], bkb[:, 0:n], v_, ALU.mult, [bonblk], [bkb, ppt])
                          P.act(bonblk[:, pr, 0:n], bonblk[:, pr, 0:n], AF.Identity, [bonblk], [bonblk, rv], bias=rv[:, 20 + pr:21 + pr])
                          lg_ = sc[6]
                          P.act(lg_[:, 0:n], sg_[:, 0:n], AF.Copy, [lg_], [sg_], scale=-DECAY_C)
                          csm = sc[7]
                          S.op("dve", lambda e, o=csm[:, 0:n], d0=rmask[:, 0:n], d1=lg_[:, 0:n]: e.tensor_tensor_scan(
                              out=o, data0=d0, data1=d1, initial=0.0, op0=ALU.mult, op1=ALU.add),
                              reads=[rmask.b, lg_.b], writes=[csm.b])
                          cse = sc[0]
                          P.tt("pool", cse[:, 0:n], csm[:, 0:n], lg_[:, 0:n], ALU.subtract, [cse], [csm, lg_])
                          wi_ = sc[8]
                          P.act(wi_[:, 0:n], csm[:, 0:n], AF.Exp, [wi_], [csm])
                          P.copy("pool", Wend[:, pr, 0:n // 64].rearrange("p (c o) -> p c o", o=1),
                                 wi_[:, 0:n].rearrange("p (c t) -> p c t", t=64)[:, :, 63:64], [Wend], [wi_])
                          winv = sc[2]
                          P.act(winv[:, 0:n], csm[:, 0:n], AF.Exp, [winv], [csm], scale=-1.0)
                          we_ = sc[6]
                          P.act(we_[:, 0:n], cse[:, 0:n], AF.Exp, [we_], [cse])
                          P.tt("dve", RhA[:, pr, 0:n], r_, wi_[:, 0:n], ALU.mult, [RhA], [ppt, wi_])
                          P.tt("dve", KKA[:, pr, 0:n], kk_[:, 0:n], we_[:, 0:n], ALU.mult, [KKA], [kk_, we_])
                          P.tt("pool", BhA[:, pr, 0:n], b_[:, 0:n], winv[:, 0:n], ALU.mult, [BhA], [b_, winv])
                          P.tt("dve", KhA[:, pr, 0:n], k2_[:, 0:n], winv[:, 0:n], ALU.mult, [KhA], [k2_, winv])
                          P.copy("act", VbA[:, pr, 0:n], v_, [VbA], [ppt])
                          if bi == 1 and pr == 0:
                              P.dbg("a", a_[:, 0:n], [128, 512], [a_])
                              P.dbg("kk", kk_[:, 0:n], [128, 512], [kk_])
                              P.dbg("k2", k2_[:, 0:n], [128, 512], [k2_])
                              P.dbg("r", ppt[:, 0, 0:n], [128, 512], [ppt])
                              P.dbg("cs", csm[:, 0:n], [128, 512], [csm])

                      if bi == 0:
                          P.stop(3)
                      for gi in range(n // 128):
                          if gi == min(1, n // 128 - 1) and bi + 1 < len(blocks):
                              nt0, nn = blocks[bi + 1]
                              load_x_and_norm(xb, sq, rstd, tmp, hbs[(bi + 1) % 2], nt0 + 1, nn, Am, 0, do_load=False)
                              if bi + 2 < len(blocks):
                                  load_x(xb, blocks[bi + 2][0] + 1, blocks[bi + 2][1])
                          o = gi * 128
                          sl = slice(o, o + 128)
                          last_group = (ngroup == NT // 128 - 1)
                          if last_group:
                              ccv = cc_in.rearrange("p (a c) -> p a c", c=256)
                              P.dma("pool", None, ccv[:, :, 0:128], ZQ[zcur][:], rd=[ZQ[zcur]], dwr=["cc_in"])
                              P.dma("pool", None, ccv[:, :, 128:256], ZP[zcur][:], rd=[ZP[zcur]], dwr=["cc_in"])
                          for pr in range(4):
                              for half, (q0, q1) in enumerate(((VbA, KKA), (BhA, KhA))):
                                  bk = nextbank()
                                  P.mm(bk, bk[:, 0:256], q0[:, pr, sl], eab[:], True, False, [q0, eab])
                                  P.mm(bk, bk[:, 256:512], q1[:, pr, sl], eab[:], False, True, [q1, eab])
                                  P.copy("act" if pr % 2 else "dve", TM[pr][:, half * 512:(half + 1) * 512], bk[:], [TM[pr]], [bk])
                          P.stop(5)
                          for pr in range(4):
                              for sd in range(2):
                                  ps_ = slice(sd * 64, (sd + 1) * 64)
                                  bk = nextbank()
                                  combos = ((KKA, BhA), (BhA, KKA), (KhA, KKA), (BhA, RhA))
                                  for ci_, (lh, rh) in enumerate(combos):
                                      P.mm(bk, bk[:, ci_ * 128:(ci_ + 1) * 128], lh[ps_, pr, sl], rh[ps_, pr, sl], ci_ == 0, True,
                                           [lh, rh], signal=(ci_ == 3))
                                  P.tt("dve", AMh[pr][sd][:], bk[:], msk[:, 0, :], ALU.mult, [AMh[pr][sd]], [bk, msk])
                          for sd in range(2):
                              ps_ = slice(sd * 64, (sd + 1) * 64)
                              bk = nextbank()
                              for pr in range(4):
                                  P.mm(bk, bk[:, pr * 128:(pr + 1) * 128], KhA[ps_, pr, sl], RhA[ps_, pr, sl], pr == 0, True,
                                       [KhA, RhA], signal=(pr == 3))
                              P.tt("dve", AK[sd][:].rearrange("p a c -> p (a c)"), bk[:], msk[:, 1, :], ALU.mult, [AK[sd]], [bk, msk])
                          P.stop(6)
                          bk = nextbank()
                          for pr in range(4):
                              for sd in range(2):
                                  vcol = 0 if sd == 0 else 192
                                  P.mm(bk, bk[:, pr * 128 + sd * 64: pr * 128 + sd * 64 + 64],
                                       AMh[pr][sd][:, 256:384], TM[pr][:, vcol:vcol + 64],
                                       pr == 0 and sd == 0, True, [AMh[pr][sd], TM[pr]], signal=(pr == 3 and sd == 1))
                          P.copy("act", AVs[:].rearrange("p a c -> p (a c)"), bk[:], [AVs], [bk])
                          P.stop(7)
                          def Pk(lvl, pr, sd):
                              if lvl == 0:
                                  return AMh[pr][sd], AMh[pr][sd][:, 0:128]
                              t_ = PQ[lvl % 2][pr]
                              return t_, t_[:, sd * 256:sd * 256 + 128]

                          def Qk(lvl, pr, sd):
                              if lvl == 0:
                                  return AMh[pr][sd], AMh[pr][sd][:, 128:256]
                              t_ = PQ[lvl % 2][pr]
                              return t_, t_[:, sd * 256 + 128:sd * 256 + 256]

                          for hp in range(2):
                              for j in range(2):
                                  pr = hp * 2 + j
                                  for sd in range(2):
                                      gc = slice(j * 256 + sd * 128, j * 256 + (sd + 1) * 128)
                                      P.tt("pool", GG[0][hp][:, gc], AMh[pr][sd][:, 128:256], ii2[:, 0:128], ALU.add,
                                           [GG[0][hp]], [AMh[pr][sd], ii2])
                          def g_update(lvl):
                              src, dst = (lvl - 1) % 2, lvl % 2
                              for hp in range(2):
                                  bk = nextbank()
                                  fst = True
                                  for j in range(2):
                                      pr = hp * 2 + j
                                      for sd in range(2):
                                          gc = slice(j * 256 + sd * 128, j * 256 + (sd + 1) * 128)
                                          pt, pa = Pk(lvl, pr, sd)
                                          P.mm(bk, bk[:, gc], pa, GG[src][hp][:, gc], fst, True,
                                               [pt, GG[src][hp]], signal=(j == 1 and sd == 1))
                                          fst = False
                                  P.tt("dve", GG[dst][hp][:], bk[:], GG[src][hp][:], ALU.add, [GG[dst][hp]], [bk, GG[src][hp]])

                          for lvl in range(1, 6):
                              src, dst = (lvl - 1) % 2, lvl % 2
                              for pr in range(4):
                                  bk = nextbank()
                                  fst = True
                                  for sd in range(2):
                                      pt, pa = Pk(lvl - 1, pr, sd)
                                      qt, qa = Qk(lvl - 1, pr, sd)
                                      last = (sd == 1)
                                      P.mm(bk, bk[:, sd * 256:sd * 256 + 128], qa, pa, fst, True, [pt, qt],
                                           signal=(last and lvl == 5))
                                      fst = False
                                      if lvl < 5:
                                          P.mm(bk, bk[:, sd * 256 + 128:sd * 256 + 256], pa, qa, False, True, [pt, qt], signal=last)
                                  ev_eng = "act"
                                  if lvl < 5:
                                      P.copy(ev_eng, PQ[dst][pr][:], bk[:], [PQ[dst][pr]], [bk])
                                  else:
                                      P.copy(ev_eng, PQ[dst][pr][:].rearrange("p (a c) -> p a c", c=256)[:, :, 0:128],
                                             bk[:].rearrange("p (a c) -> p a c", c=256)[:, :, 0:128], [PQ[dst][pr]], [bk])
                              if lvl >= 2:
                                  g_update(lvl - 1)
                          g_update(5)
                          P.stop(8)
                          gfin = GG[5 % 2]
                          for hp in range(2):
                              bk = nextbank()
                              fst = True
                              for j in range(2):
                                  pr = hp * 2 + j
                                  for sd in range(2):
                                      gc = slice(j * 256 + sd * 128, j * 256 + (sd + 1) * 128)
                                      kcol = 256 if sd == 0 else 448
                                      oc = j * 256 + sd * 128
                                      P.mm(bk, bk[:, oc:oc + 64], gfin[hp][:, gc], TM[pr][:, kcol:kcol + 64], fst, True,
                                           [gfin[hp], TM[pr]], signal=False)
                                      fst = False
                                      P.mm(bk, bk[:, oc + 64:oc + 128], gfin[hp][:, gc], AVs[:, pr, sd * 64:(sd + 1) * 64], False, True,
                                           [gfin[hp], AVs], signal=(j == 1 and sd == 1))
                              for j in range(2):
                                  pr = hp * 2 + j
                                  src_a = bk[:, j * 256:j * 256 + 128].rearrange("p (a c) -> p a c", c=64)
                                  dst_a = KU[pr][:, 0:256].rearrange("p (a c) -> p a c", c=128)[:, :, 0:64]
                                  src_b = bk[:, j * 256 + 128:j * 256 + 256].rearrange("p (a c) -> p a c", c=64)
                                  dst_b = KU[pr][:, 256:512].rearrange("p (a c) -> p a c", c=128)[:, :, 64:128]
                                  if hp == 0:
                                      P.ts("dve", dst_a, src_a, -1.0, None, ALU.mult, None, [KU[pr]], [bk])
                                      P.ts("dve", dst_b, src_b, -1.0, None, ALU.mult, None, [KU[pr]], [bk])
                                  else:
                                      P.act(dst_a, src_a, AF.Copy, [KU[pr]], [bk], scale=-1.0)
                                      P.act(dst_b, src_b, AF.Copy, [KU[pr]], [bk], scale=-1.0)
                          P.stop(9)
                          bk = nextbank()
                          for pr in range(4):
                              for sd in range(2):
                                  P.mm(bk, bk[:, pr * 128:(pr + 1) * 128], KU[pr][:, sd * 256:sd * 256 + 128],
                                       AMh[pr][sd][:, 384:512], pr == 0 and sd == 0, sd == 1, [KU[pr], AMh[pr][sd]],
                                       signal=(pr == 3 and sd == 1))
                          P.tt("dve", RT[:], bk[:].rearrange("p (a c) -> p a c", c=128), RhA[:, :, sl], ALU.add, [RT], [bk, RhA])
                          P.stop(10)
                          for c in range(2):
                              rows = slice(c * 64, (c + 1) * 64)
                              bk = nextbank()
                              for pr in range(4):
                                  for sd in range(2):
                                      bcol = 512 if sd == 0 else 704
                                      P.mm(bk, bk[:, pr * 128 + sd * 64:pr * 128 + sd * 64 + 64], KU[pr][rows, sd * 256:sd * 256 + 128],
                                           TM[pr][rows, bcol:bcol + 64], pr == 0 and sd == 0, True, [KU[pr], TM[pr]], signal=True)
                              P.tt("dve", MT[c][:].rearrange("p a c -> p (a c)"), bk[:], ii4[:], ALU.add, [MT[c]], [bk, ii4])
                              bk2 = nextbank()
                              fst = True
                              for pr in range(4):
                                  for sd in range(2):
                                      oc = pr * 128 + sd * 64
                                      ucol = 128 if sd == 0 else 448
                                      vcol = 0 if sd == 0 else 192
                                      P.mm(bk2, bk2[:, oc:oc + 64], TM[pr][rows, 512 + sd * 128:512 + (sd + 1) * 128],
                                           KU[pr][rows, ucol:ucol + 64], fst, False, [TM[pr], KU[pr]], signal=False)
                                      fst = False
                                      P.mm(bk2, bk2[:, oc:oc + 64], TM[pr][rows, 768 + sd * 128:768 + (sd + 1) * 128],
                                           TM[pr][rows, vcol:vcol + 64], False, True, [TM[pr]], signal=True)
                              ci = gi * 2 + c
                              for pr in range(4):
                                  if True:
                                      P.act(N0[c][:, pr, :], bk2[:, pr * 128:(pr + 1) * 128], AF.Copy, [N0[c]], [bk2, Wend],
                                            scale=Wend[:, pr, ci:ci + 1])
                                  else:
                                      P.ts("dve", N0[c][:, pr, :], bk2[:, pr * 128:(pr + 1) * 128], Wend[:, pr, ci:ci + 1], None,
                                           ALU.mult, None, [N0[c]], [bk2, Wend])
                          P.stop(11)
                          bky = nextbank()
                          bkr = nextbank()
                          fy = True
                          for pr in range(4):
                              yc = slice(pr * 128, (pr + 1) * 128)
                              for (lh, rh) in ((KU[pr][:, 128:256], AMh[pr][0][:, 384:512]), (KU[pr][:, 384:512], AMh[pr][1][:, 384:512]),
                                               (TM[pr][:, 0:128], AK[0][:, pr, :]), (TM[pr][:, 128:256], AK[1][:, pr, :])):
                                  P.mm(bky, bky[:, yc], lh, rh, fy, False, [KU[pr], AMh[pr][0], AMh[pr][1], AK[0], AK[1], TM[pr]],
                                       signal=False)
                                  fy = False
                          fr = True
                          for c in range(2):
                              ci = gi * 2 + c
                              zq, zp, znq, znp = ZQ[zcur], ZP[zcur], ZQ[1 - zcur], ZP[1 - zcur]
                              for pr in range(4):
                                  cc_ = slice(pr * 128 + c * 64, pr * 128 + c * 64 + 64)
                                  P.mm(bky, bky[:, cc_], zq[:, pr, :], RT[:, pr, c * 64:(c + 1) * 64], False, True, [zq, RT],
                                       signal=True)
                                  P.mm(bkr, bkr[:, cc_], zp[:, pr, :], RT[:, pr, c * 64:(c + 1) * 64], fr, True, [zp, RT],
                                       signal=True)
                                  fr = False
                              bkq, bkp = nextbank(), nextbank()
                              for pr in range(4):
                                  P.mm(bkq, bkq[:, pr * 128:(pr + 1) * 128], MT[c][:, pr, :], zq[:, pr, :], pr == 0, True, [MT[c], zq],
                                       signal=(pr == 3))
                              for pr in range(4):
                                  P.mm(bkp, bkp[:, pr * 128:(pr + 1) * 128], MT[c][:, pr, :], zp[:, pr, :], pr == 0, True, [MT[c], zp],
                                       signal=(pr == 3))
                              for pr in range(4):
                                  dsc = Wend[:, pr, ci:ci + 1]
                                  P.stt(znq[:, pr, :], bkq[:, pr * 128:(pr + 1) * 128], dsc, N0[c][:, pr, :], ALU.mult, ALU.add,
                                        [znq], [bkq, Wend, N0[c]])
                                  P.act(znp[:, pr, :], bkp[:, pr * 128:(pr + 1) * 128], AF.Copy, [znp], [bkp, Wend], scale=dsc)
                              zcur = 1 - zcur
                          P.copy("act", yqblk[:, :, sl], bky[:].rearrange("p (a c) -> p a c", c=128), [yqblk], [bky])
                          P.copy("dve", rpblk[:, :, sl], bkr[:].rearrange("p (a c) -> p a c", c=128), [rpblk], [bkr])
                          ngroup += 1

                      if bi == 0:
                          P.stop(4)
                      for (dn, dt_, tl) in (("yq_s", yq_s, yqblk), ("rp_s", rp_s, rpblk), ("g_s", g_s, gblk), ("bon_s", bon_s, bonblk)):
                          P.dma("sp", sem_o, dt_.rearrange("a p t -> p a t")[:, :, t0:t0 + n], tl[:, :, 0:n], rd=[tl], dwr=[dn])
                except StopBuild:
                    pass
                S.wait_all("sp", [P.dbuf[nm] for nm in ("yq_s", "rp_s", "g_s", "bon_s", "cc_in")])
                if "yq" in P.debug:
                    pass
                S.emit()

        if "yq" in P.debug:
            with contextlib.ExitStack() as st:
                tl = P.sb(st, "dbgt", [128, 4, NT])
                o = P.dout("dbg_yq", [128, 4, NT])
                P.dma("sp", sem_o2, tl[:], yq_s.rearrange("a p t -> p a t"), wr=[tl], drd=["yq_s"])
                P.dma("sp", sem_o2, o, tl[:], rd=[tl], dwr=["dbg_yq"])
                S.wait_all("sp", [P.dbuf["dbg_yq"]])
                S.emit()

        st = contextlib.ExitStack()
        xb_stack = st
        if "X" in phases:
            if True:
                zall = P.sb(st, "zall", [128, 4, 1024])
                zs = P.sb(st, "zs", [128, 4, 128])
                zsb = P.sb(st, "zsb", [128, 4, 128], BF16)
                zpb = P.sb(st, "zpb", [128, 4, 128], BF16)
                zpt = P.sb(st, "zpt", [128, 4, 128], BF16)
                xt1 = P.sb(st, "xt1", [128, 4, 128])
                S.sems["cc"] = nc.alloc_semaphore(name="cc_sem")
                S.raw("pool", [], lambda e: e.collective_compute(
                    "AllGather", ALU.bypass, replica_groups=[[0, 1, 2, 3], [4, 5, 6, 7]],
                    ins=[P.dram["cc_in"].ap().opt()], outs=[P.dram["cc_out"].ap().opt()]), "cc", 1)
                P.dbuf["cc_out"].lw = ("cc", 1, "dma", 0)
                P.dma("pool", None, zall[:], cc_out.rearrange("(r p) f -> p r f", p=128), wr=[zall], drd=["cc_out"])
                def emit_fold():
                  P.memset("dve", zs[:], 0.0, [zs])
                  for c in range(3):
                      zv = zall[:, c, :].rearrange("p (a c) -> p a c", c=256)
                      P.copy("dve", zpb[:], zv[:, :, 128:256], [zpb], [zall])
                      P.copy("act", zsb[:], zs[:], [zsb], [zs])
                      bk = nextbank()
                      for pr in range(4):
                          P.mm(bk, bk[:, pr * 128:(pr + 1) * 128], zpb[:, pr, :], identb[:], pr == 0, True, [zpb, identb],
                               signal=(pr == 3))
                      P.copy("act", zpt[:].rearrange("p a c -> p (a c)"), bk[:], [zpt], [bk])
                      bk2 = nextbank()
                      for pr in range(4):
                          P.mm(bk2, bk2[:, pr * 128:(pr + 1) * 128], zpt[:, pr, :], zsb[:, pr, :], pr == 0, True, [zpt, zsb],
                               signal=(pr == 3))
                      P.tt("dve", xt1[:], bk2[:].rearrange("p (a c) -> p a c", c=128), zv[:, :, 0:128], ALU.add, [xt1], [bk2, zall])
                      P.tt("dve", xt1[:], xt1[:], zs[:], ALU.subtract, [xt1], [xt1, zs])
                      P.stt(zs[:], xt1[:], cmask[:, c:c + 1], zs[:], ALU.mult, ALU.add, [zs], [xt1, cmask, zs])
                  P.copy("dve", zst[:], zs[:], [zst], [zs])
                  P.dbg("zs", zs[:], [128, 4, 128], [zs])
                if "B" not in phases:
                    emit_fold()
                    S.emit()
                    xb_stack.close()

        if "B" in phases:
            if True:
                wcv = P.sb(st, "wcv", [128, 8, 1024], BF16)
                wout = P.sb(st, "wout", [128, 8, 1024], BF16)
                dgt = [[P.sb(st, "dgt%d_%d" % (i, e), [128, 16 - e, 128], BF16) for e in range(2)] for i in range(4)]
                cwv = P.sb(st, "cwv", [128, 4, 31])
                cvv = P.sb(st, "cvv", [128, 12])
                rvb = P.sb(st, "rvb", [128, 28])
                xb0 = P.sb(st, "xb2", [128, 8, 512])
                xb1 = T(zall.t[:].rearrange("p a (k c) -> p (a k) c", c=512), "xb_alias")
                xb1.b = zall.b
                xbs = [xb0, xb1]
                rstd = P.sb(st, "brstd", [128, 512])
                tmp = [P.sb(st, "btmp%d" % i, [128, 512]) for i in range(2)]
                hbs = [P.sb(st, "bhb%d" % i, [128, 8, 512], BF16) for i in range(2)]
                UGs = [P.sb(st, "UG%d" % i, [128, 4, 30 + 512], BF16) for i in range(2)]
                sigt = [P.sb(st, "sigt%d" % i, [128, 512], BF16) for i in range(2)]
                CB = P.sb(st, "CB", [128, 4, 512])
                cbb = [P.sb(st, "cbb%d" % i, [128, 512], BF16) for i in range(2)]
                csq = [P.sb(st, "csq%d" % i, [128, 512], BF16) for i in range(2)]
                lnm = P.sb(st, "lnm", [128, 512])
                lnq = P.sb(st, "lnq", [128, 512])
                lnr = P.sb(st, "lnr", [128, 512])
                UO = P.sb(st, "UO", [128, 4, 512], BF16)
                yqb = P.sb(st, "yqb", [128, 4, 512], BF16)
                rpb = P.sb(st, "rpb", [128, 4, 512], BF16)
                gb = P.sb(st, "gb", [128, 4, 512], BF16)
                bonb = P.sb(st, "bonb", [128, 4, 512], BF16)
                yv = [P.sb(st, "yv%d" % i, [128, 512]) for i in range(2)]
                ybf = [P.sb(st, "ybf%d" % i, [128, 512], BF16) for i in range(2)]
                ysq = [P.sb(st, "ysq%d" % i, [128, 512], BF16) for i in range(2)]
                sq = ysq
                ym, yq_, yr = lnm, lnq, lnr
                YO = P.sb(st, "YO", [128, 4, 512], BF16)
                msb = P.sb(st, "msb", [128, 8, 512])
                msbv = [T(msb.t[:, ot, :], "msb_ot%d" % ot) for ot in range(8)]

                load_x(xbs[0], blocks[0][0] + 1, blocks[0][1])
                load_x_and_norm(xbs[0], sigt, rstd, tmp, hbs[0], blocks[0][0] + 1, blocks[0][1], Am, 0, do_load=False)
                wst = [T(msb.t[:, 2 * i:2 * i + 2, :].rearrange("p a c -> p (a c)"), "wstB%d" % i) for i in range(4)]
                P.dma("sp", None, cwv[:], cw, wr=[cwv])
                P.dma("sp", None, cvv[:], cvecs, wr=[cvv])
                P.dma("sp", None, rvb[:], rwv, wr=[rvb])
                nst = 0
                for (dst_, srcw, c_lo) in ((wcv, w_in, 0), (wout, w_out, 0)):
                    for kt in range(8):
                        stg_ = wst[nst % 4]
                        P.dma("sp", None, stg_[:], srcw[kt * 128:(kt + 1) * 128, c_lo:c_lo + 1024], wr=[stg_])
                        P.copy("act" if nst % 2 else "dve", dst_[:, kt, :], stg_[:], [dst_], [stg_])
                        nst += 1
                    if dst_ is wcv:
                        for ct in range(4):
                            for k in range(31):
                                dg_ = dgt[ct][k % 2]
                                if k % 2 == 0:
                                    P.ts("dve", dg_[:, k // 2, :], identb[:], cwv[:, ct, k:k + 1], None, ALU.mult, None,
                                         [dg_], [identb, cwv])
                                else:
                                    P.act(dg_[:, k // 2, :], identb[:], AF.Copy, [dg_], [identb, cwv], scale=cwv[:, ct, k:k + 1])
                P.memset("dve", UGs[0][:], 0.0, [UGs[0]])
                P.memset("pool", UGs[1][:], 0.0, [UGs[1]])

                try:
                  P.stop(20)
                  def front_norm(bi):
                    t0, n = blocks[bi]
                    load_x_and_norm(xbs[bi % 2], sigt, rstd, tmp, hbs[bi % 2], t0 + 1, n, Am, 0, do_load=False)

                  def front(bi):
                    t0, n = blocks[bi]
                    xb, hb, UG = xbs[bi % 2], hbs[bi % 2], UGs[bi % 2]
                    if bi >= 1:
                        npv = blocks[bi - 1][1]
                        P.copy("pool", UG[:, :, 0:30], UGs[(bi - 1) % 2][:, :, npv:npv + 30], [UG], [UGs[(bi - 1) % 2]])
                    for ct in range(4):
                        bu, bg = nextbank(), nextbank()
                        for kt in range(8):
                            P.mm(bu, bu[:, 0:n], wcv[:, kt, ct * 128:(ct + 1) * 128], hb[:, kt, 0:n], kt == 0, kt == 7, [wcv, hb])
                        for kt in range(8):
                            P.mm(bg, bg[:, 0:n], wcv[:, kt, 512 + ct * 128:512 + (ct + 1) * 128], hb[:, kt, 0:n], kt == 0, kt == 7,
                                 [wcv, hb])
                        sg_ = sigt[ct % 2]
                        P.act(sg_[:, 0:n], bg[:, 0:n], AF.Sigmoid, [sg_], [bg])
                        if bi == 0:
                            P.stt(UG[:, ct, 30:30 + n], bu[:, 0:n], hv[:, 0:1], sg_[:, 0:n], ALU.mult, ALU.mult, [UG], [bu, hv, sg_])
                        else:
                            P.tt("dve", UG[:, ct, 30:30 + n], bu[:, 0:n], sg_[:, 0:n], ALU.mult, [UG], [bu, sg_])

                  def load_y(bj):
                    t0_, n_ = blocks[bj]
                    tsl = slice(t0_, t0_ + n_)
                    P.dma("sp", None, yqb[:, :, 0:n_], yq_s.rearrange("a p t -> p a t")[:, :, tsl], wr=[yqb], drd=["yq_s"])
                    P.dma("sp", None, rpb[:, :, 0:n_], rp_s.rearrange("a p t -> p a t")[:, :, tsl], wr=[rpb], drd=["rp_s"])
                    P.dma("sp", None, gb[:, :, 0:n_], g_s.rearrange("a p t -> p a t")[:, :, tsl], wr=[gb], drd=["g_s"])
                    P.dma("sp", None, bonb[:, :, 0:n_], bon_s.rearrange("a p t -> p a t")[:, :, tsl], wr=[bonb], drd=["bon_s"])

                  def back1(bi):
                    t0, n = blocks[bi]
                    UG = UGs[bi % 2]
                    if bi == 0:
                        emit_fold()
                        load_x(xbs[1], blocks[1][0] + 1, blocks[1][1])
                    if bi == 0:
                        load_y(0)
                    bs1, bs2 = nextbank(True), nextbank(True)
                    bsb[0], bsb[1] = bs1, bs2
                    def conv_ct(ct):
                        bc = nextbank()
                        for k in range(31):
                            P.mm(bc, bc[:, 0:n], dgt[ct][k % 2][:, k // 2, :], UG[:, ct, k:k + n], k == 0, k == 30, [dgt[ct][k % 2], UG])
                        P.act(CB[:, ct, 0:n], bc[:, 0:n], AF.Identity, [CB], [bc, cvv], bias=cvv[:, ct:ct + 1])
                        P.act(csq[ct % 2][:, 0:n], bc[:, 0:n], AF.Square, [csq[ct % 2]], [bc, cvv], bias=cvv[:, ct:ct + 1])
                        P.copy("pool", cbb[ct % 2][:, 0:n], CB[:, ct, 0:n], [cbb[ct % 2]], [CB])

                    def conv_stats(ct):
                        P.mm(bs1, bs1[:, 0:n], onesb[:], cbb[ct % 2][:, 0:n], ct == 0, ct == 3, [onesb, cbb[ct % 2]], signal=True)
                        P.mm(bs2, bs2[:, 0:n], onesb[:], csq[ct % 2][:, 0:n], ct == 0, ct == 3, [onesb, csq[ct % 2]], signal=True)

                    def y_a(pr):
                        by = nextbank()
                        y_ = yv[pr % 2]
                        P.mm(by, by[:, 0:n], zst[:, pr, :], rpb[:, pr, 0:n], True, True, [zst, rpb])
                        P.tt("dve", y_[:, 0:n], by[:, 0:n], yqb[:, pr, 0:n], ALU.add, [y_], [by, yqb])
                        P.act(ysq[pr % 2][:, 0:n], y_[:, 0:n], AF.Square, [ysq[pr % 2]], [y_])
                        P.copy("act", ybf[pr % 2][:, 0:n], y_[:, 0:n], [ybf[pr % 2]], [y_])

                    def y_b(pr):
                        b1, b2 = nextbank(), nextbank()
                        y_ = yv[pr % 2]
                        P.mm(b1, b1[:, 0:n], blkb[:], ybf[pr % 2][:, 0:n], True, True, [blkb, ybf[pr % 2]])
                        P.mm(b2, b2[:, 0:n], blkb[:], ysq[pr % 2][:, 0:n], True, True, [blkb, ysq[pr % 2]])
                        P.act(ym[:, 0:n], b1[:, 0:n], AF.Copy, [ym], [b1], scale=1.0 / 64)
                        P.tt("dve", yq_[:, 0:n], ym[:, 0:n], ym[:, 0:n], ALU.mult, [yq_], [ym])
                        P.stt(yq_[:, 0:n], b2[:, 0:n], 1.0 / 64, yq_[:, 0:n], ALU.mult, ALU.subtract, [yq_], [b2, yq_])
                        P.rsqrt(yr[:, 0:n], yq_[:, 0:n], LNX_EPS, 1.0, yr, [yq_])
                        P.tt("dve", y_[:, 0:n], y_[:, 0:n], ym[:, 0:n], ALU.subtract, [y_], [y_, ym])
                        P.tt("pool", y_[:, 0:n], y_[:, 0:n], yr[:, 0:n], ALU.mult, [y_], [y_, yr])
                        P.stt(y_[:, 0:n], y_[:, 0:n], rvb[:, 16 + pr:17 + pr], bonb[:, pr, 0:n], ALU.mult, ALU.add, [y_], [y_, rvb, bonb])
                        P.tt("dve", YO[:, pr, 0:n], y_[:, 0:n], gb[:, pr, 0:n], ALU.mult, [YO], [y_, gb])

                    for i_ in range(4):
                        conv_ct(i_)
                        if i_ >= 1:
                            conv_stats(i_ - 1)
                            y_b(i_ - 1)
                        y_a(i_)
                        if i_ == 1 and bi + 1 < len(blocks):
                            front_norm(bi + 1)
                    conv_stats(3)
                    y_b(3)

                  def back2(bi):
                    t0, n = blocks[bi]
                    xb = xbs[bi % 2]
                    bs1, bs2 = bsb
                    if bi + 1 < len(blocks):
                        load_y(bi + 1)
                    P.act(lnm[:, 0:n], bs1[:, 0:n], AF.Copy, [lnm], [bs1], scale=1.0 / 512)
                    P.tt("dve", lnq[:, 0:n], lnm[:, 0:n], lnm[:, 0:n], ALU.mult, [lnq], [lnm])
                    P.stt(lnq[:, 0:n], bs2[:, 0:n], 1.0 / 512, lnq[:, 0:n], ALU.mult, ALU.subtract, [lnq], [bs2, lnq])
                    release(bs1)
                    release(bs2)
                    P.rsqrt(lnr[:, 0:n], lnq[:, 0:n], LN_EPS, 1.0, lnr, [lnq])
                    for ct in range(4):
                        t_ = tmp[ct % 2]
                        P.tt("dve", t_[:, 0:n], CB[:, ct, 0:n], lnm[:, 0:n], ALU.subtract, [t_], [CB, lnm])
                        P.tt("pool", t_[:, 0:n], t_[:, 0:n], lnr[:, 0:n], ALU.mult, [t_], [t_, lnr])
                        P.act(UO[:, ct, 0:n], t_[:, 0:n], AF.Silu, [UO], [t_, cvv], bias=cvv[:, 8 + ct:9 + ct], scale=cvv[:, 4 + ct:5 + ct])
                    if bi == 1:
                        P.dbg("uo", UO[:, 0, 0:n], [128, 512], [UO], BF16)
                    if bi == 1:
                        P.dbg("yo", YO[:, 0, 0:n], [128, 512], [YO], BF16)
                    P.stop(25)
                    bss = nextbank(True)
                    for ot in range(8):
                        bm = nextbank()
                        for kt in range(4):
                            P.mm(bm, bm[:, 0:n], wout[:, kt, ot * 128:(ot + 1) * 128], UO[:, kt, 0:n], kt == 0, False, [wout, UO])
                        for kt in range(4):
                            P.mm(bm, bm[:, 0:n], wout[:, 4 + kt, ot * 128:(ot + 1) * 128], YO[:, kt, 0:n], False, kt == 3, [wout, YO])
                        P.act(sq[ot % 2][:, 0:n], bm[:, 0:n], AF.Square, [sq[ot % 2]], [bm])
                        P.copy("dve", msb[:, ot, 0:n], bm[:, 0:n], [msbv[ot]], [bm])
                        if ot >= 1:
                            P.mm(bss, bss[:, 0:n], onesb[:], sq[(ot - 1) % 2][:, 0:n], ot == 1, False, [onesb, sq[(ot - 1) % 2]], signal=True)
                    P.mm(bss, bss[:, 0:n], onesb[:], sq[7 % 2][:, 0:n], False, True, [onesb, sq[7 % 2]], signal=True)
                    release(bss)
                    P.stop(26)
                    P.rsqrt(rstd[:, 0:n], bss[:, 0:n], float(D * RMS_EPS), 1.0, rstd, [bss])
                    for ot in range(8):
                        P.tt("pool" if ot % 2 else "dve", msb[:, ot, 0:n], msb[:, ot, 0:n], rstd[:, 0:n], ALU.mult, [msbv[ot]], [msbv[ot], rstd])
                        P.stt(msb[:, ot, 0:n], msb[:, ot, 0:n], Gm[:, ot:ot + 1], xb[:, ot, 0:n], ALU.mult, ALU.add, [msbv[ot]], [msbv[ot], Gm, xb])
                    xmv = xm_s.rearrange("(kt p) t -> p kt t", p=128)
                    if bi == 0:
                        P.dma("sp", None, xmv[:, :, 0:2], msb[:, :, n - 2:n], rd=[msb] + msbv, dwr=["xm_s"])
                    else:
                        P.dma("sp", None, xmv[:, :, 2 + t0 - HALO:2 + t0 - HALO + n], msb[:, :, 0:n], rd=[msb] + msbv, dwr=["xm_s"])
                    if bi + 2 < len(blocks):
                        load_x(xbs[bi % 2], blocks[bi + 2][0] + 1, blocks[bi + 2][1])

                  bsb = [None, None]
                  front(0)
                  for bi in range(len(blocks)):
                      back1(bi)
                      if bi + 1 < len(blocks):
                          front(bi + 1)
                      back2(bi)
                except StopBuild:
                    for b_ in list(held):
                        held.discard(b_)
                S.wait_all("sp", [P.dbuf["xm_s"]])
                S.emit()
                xb_stack.close()

        if "xm" in P.debug:
            with contextlib.ExitStack() as st:
                tl = P.sb(st, "dbgx", [128, 8, 514])
                o = P.dout("dbg_xm", [128, 8, 514])
                P.dma("sp", None, tl[:], xm_s.rearrange("(kt p) t -> p kt t", p=128)[:, :, 0:514], wr=[tl], drd=["xm_s"])
                P.dma("sp", None, o, tl[:], rd=[tl], dwr=["dbg_xm"])
                S.wait_all("sp", [P.dbuf["dbg_xm"]])
                S.emit()

        if "C" in phases:
            with contextlib.ExitStack() as st:
                wdn = P.sb(st, "wdn", [128, 22, 1024], BF16)
                wup = [[P.sb(st, "wup%d_%d" % (i, j), [128, 8, 128], BF16) for j in range(2)] for i in range(2)]
                fwv = P.sb(st, "fwv", [128, 44, 3])
                fbv = P.sb(st, "fbv", [128, 44])
                xma = P.sb(st, "xma", [128, 8, 514])
                xmb = P.sb(st, "xmb", [128, 8, 512])
                h2 = P.sb(st, "h2", [128, 8, 1026], BF16)
                Gt = P.sb(st, "Gt", [128, 22, 1024], BF16)
                ZS = [[P.sb(st, "ZS%d_%d" % (i, j), [128, 514], BF16) for j in range(2)] for i in range(2)]
                dgc = [[P.sb(st, "dgc%d_%d" % (i, j), [128, 3, 128], BF16) for j in range(2)] for i in range(2)]
                acc = [[P.sb(st, "acc%d_%d" % (i, j), [128, 512]) for j in range(2)] for i in range(2)]
                fsb = P.sb(st, "fsb", [128, 8, 512])
                fsbv = [T(fsb.t[:, ot, :], "fsb_ot%d" % ot) for ot in range(8)]
                sq = [P.sb(st, "csq2_%d" % i, [128, 512], BF16) for i in range(2)]
                rstd = acc[1][0]
                tmp = [acc[0][0], acc[0][1]]

                wst = [P.sb(st, "wstC%d" % i, [128, 1024]) for i in range(6)]
                xmv = xm_s.rearrange("(kt p) t -> p kt t", p=128)

                def load_xm(half, part):
                    if part == 0:
                        P.dma("sp", None, xma[:], xmv[:, :, half * 1024:half * 1024 + 514], wr=[xma], drd=["xm_s"])
                    else:
                        P.dma("sp", None, xmb[:], xmv[:, :, half * 1024 + 514:half * 1024 + 1026], wr=[xmb], drd=["xm_s"])

                load_xm(0, 0)
                load_xm(0, 1)
                P.dma("sp", None, fwv[:], fw, wr=[fwv])
                P.dma("sp", None, fbv[:], fb, wr=[fbv])
                wdst = [wst[4], wst[5]]
                wupv = w_up.rearrange("(kt p) n -> p kt n", p=128)
                xmv = xm_s.rearrange("(kt p) t -> p kt t", p=128)

                def norm_cols(xm, sc0, c0, ncol):
                    bk = nextbank()
                    for kt in range(8):
                        s_ = sq[kt % 2]
                        P.act(s_[:, 0:ncol], xm[:, kt, sc0:sc0 + ncol], AF.Square, [s_], [xm])
                        P.mm(bk, bk[:, 0:ncol], onesb[:], s_[:, 0:ncol], kt == 0, kt == 7, [onesb, s_], signal=True)
                    P.rsqrt(rstd[:, 0:ncol], bk[:, 0:ncol], float(D * RMS_EPS), 1.0, rstd, [bk])
                    for kt in range(8):
                        t_ = tmp[kt % 2]
                        P.tt("dve", t_[:, 0:ncol], xm[:, kt, sc0:sc0 + ncol], rstd[:, 0:ncol], ALU.mult, [t_], [xm, rstd])
                        P.act(h2[:, kt, c0:c0 + ncol], t_[:, 0:ncol], AF.Identity, [h2], [t_, Af, modv],
                              bias=modv[:, 24 + kt:25 + kt], scale=Af[:, kt:kt + 1])

                nw = 0
                for half in range(2):
                    if half == 0:
                        norm_cols(xma, 0, 0, 2)
                        norm_cols(xma, 2, 2, 512)
                        norm_cols(xmb, 0, 514, 512)
                    if half == 0:
                        P.ts("dve", h2[:, :, 0:2], h2[:, :, 0:2], hv[:, 0:1], None, ALU.mult, None, [h2], [h2, hv])
                    def w_dma(z_):
                        sg_, sv_ = wst[(z_ % 2) * 2], wst[(z_ % 2) * 2 + 1]
                        P.dma("sp", None, sg_[:].rearrange("p (k c) -> p k c", c=128), wupv[:, :, z_ * 128:(z_ + 1) * 128], wr=[sg_])
                        P.dma("sp", None, sv_[:].rearrange("p (k c) -> p k c", c=128), wupv[:, :, DFF + z_ * 128:DFF + (z_ + 1) * 128],
                              wr=[sv_])

                    def w_cast(z_):
                        sg_, sv_ = wst[(z_ % 2) * 2], wst[(z_ % 2) * 2 + 1]
                        wg_, wv_ = wup[z_ % 2]
                        P.copy("dve", wg_[:].rearrange("p k c -> p (k c)"), sg_[:], [wg_], [sg_])
                        P.copy("dve", wv_[:].rearrange("p k c -> p (k c)"), sv_[:], [wv_], [sv_])

                    w_dma(0)
                    w_dma(1)
                    w_cast(0)
                    pend = []
                    for zt in range(22):
                        wg, wv = wup[zt % 2]
                        if zt + 2 < 22:
                            w_dma(zt + 2)
                        if zt + 1 < 22:
                            w_cast(zt + 1)
                        if half == 0:
                            P.dma("sp", None, wdst[zt % 2][:], w_down[zt * 128:(zt + 1) * 128, :], wr=[wdst[zt % 2]])
                            P.copy("act", wdn[:, zt, :], wdst[zt % 2][:], [wdn], [wdst[zt % 2]])
                        zsg, zsv = ZS[zt % 2]
                        for j_, zi in ((0, zt), (1, zt + 22)):
                            for k in range(3):
                                P.ts("dve", dgc[zt % 2][j_][:, k, :], identb[:], fwv[:, zi, k:k + 1], None, ALU.mult, None,
                                     [dgc[zt % 2][j_]], [identb, fwv])
                        bh = nextbank()
                        for kt in range(8):
                            P.mm(bh, bh[:, 0:2], wg[:, kt, :], h2[:, kt, 0:2], kt == 0, kt == 7, [wg, h2])
                        for kt in range(8):
                            P.mm(bh, bh[:, 2:4], wv[:, kt, :], h2[:, kt, 0:2], False, kt == 7, [wv, h2])
                        P.copy("act", zsg[:, 0:2], bh[:, 0:2], [zsg], [bh])
                        P.copy("act", zsv[:, 0:2], bh[:, 2:4], [zsv], [bh])
                        for sb_ in range(2):
                            for (w_, zs_, j_) in ((wg, zsg, 0), (wv, zsv, 1)):
                                c0 = 2 + sb_ * 512
                                bz = nextbank()
                                for kt in range(8):
                                    P.mm(bz, bz[:], w_[:, kt, :], h2[:, kt, c0:c0 + 512], kt == 0, kt == 7, [w_, h2])
                                P.copy("act", zs_[:, 2:514], bz[:], [zs_], [bz])
                                if pend:
                                    pend.pop(0)()

                                def stage2(zt=zt, sb_=sb_, j_=j_, zs_=zs_):
                                    bc = nextbank()
                                    dg_ = dgc[zt % 2][j_]
                                    for k in range(3):
                                        P.mm(bc, bc[:], dg_[:, k, :], zs_[:, k:k + 512], k == 0, k == 2, [dg_, zs_])
                                    P.copy("pool", zs_[:, 0:2], zs_[:, 512:514], [zs_], [zs_])
                                    ag = acc[sb_][0]
                                    if j_ == 0:
                                        P.act(ag[:], bc[:], AF.Silu, [ag], [bc, fbv], bias=fbv[:, zt:zt + 1])
                                    else:
                                        P.stt(Gt[:, zt, sb_ * 512:(sb_ + 1) * 512], bc[:], fbv[:, zt + 22:zt + 23], ag[:], ALU.add, ALU.mult,
                                              [Gt], [bc, fbv, ag])
                                pend.append(stage2)
                    while pend:
                        pend.pop(0)()
                    if half == 0:
                        P.dbg("G", Gt[:, 0, 0:512], [128, 512], [Gt], BF16)
                    for sb_ in range(2):
                        bss = nextbank(True)
                        for ot in range(8):
                            bm = nextbank()
                            for zt in range(22):
                                P.mm(bm, bm[:], wdn[:, zt, ot * 128:(ot + 1) * 128], Gt[:, zt, sb_ * 512:(sb_ + 1) * 512],
                                     zt == 0, zt == 21, [wdn, Gt])
                            P.act(sq[ot % 2][:], bm[:], AF.Square, [sq[ot % 2]], [bm])
                            P.copy("dve", fsb[:, ot, :], bm[:], [fsbv[ot]], [bm])
                            if ot >= 1:
                                P.mm(bss, bss[:], onesb[:], sq[(ot - 1) % 2][:], ot == 1, False, [onesb, sq[(ot - 1) % 2]], signal=True)
                        P.mm(bss, bss[:], onesb[:], sq[7 % 2][:], False, True, [onesb, sq[7 % 2]], signal=True)
                        release(bss)
                        if half == 0 and sb_ == 1:
                            norm_cols(xma, 0, 0, 2)
                            norm_cols(xma, 2, 2, 512)
                        P.rsqrt(rstd[:], bss[:], float(D * RMS_EPS), 1.0, rstd, [bss])
                        xsrc = xma[:, :, 2:514] if sb_ == 0 else xmb[:, :, 0:512]
                        xt_ = xma if sb_ == 0 else xmb
                        for ot in range(8):
                            P.tt("pool" if ot % 2 else "dve", fsb[:, ot, :], fsb[:, ot, :], rstd[:], ALU.mult, [fsbv[ot]], [fsbv[ot], rstd])
                            P.stt(fsb[:, ot, :], fsb[:, ot, :], Gf[:, ot:ot + 1], xsrc[:, ot, :],
                                  ALU.mult, ALU.add, [fsbv[ot]], [fsbv[ot], Gf, xt_])
                        oc = half * 1024 + sb_ * 512
                        P.dma("sp", None, outT.rearrange("(kt p) t -> p kt t", p=128)[:, :, oc:oc + 512], fsb[:], rd=[fsb] + fsbv, dwr=["outT"])
                        if half == 0:
                            load_xm(1, sb_)
                            if sb_ == 1:
                                norm_cols(xmb, 0, 514, 512)
                S.wait_all("sp", [P.dbuf["outT"]])
                S.emit()

        P.finish_outputs = []
        if "C" not in phases:
            with contextlib.ExitStack() as st:
                zt = P.sb(st, "zt", [128, TOK])
                P.memset("dve", zt[:], 0.0, [zt])
                for kt in range(8):
                    P.dma("sp", sem_o2, outT[kt * 128:(kt + 1) * 128, :], zt[:], rd=[zt], dwr=["outT"])
                S.wait_all("sp", [P.dbuf["outT"]])
                S.emit()
    return P


def _tiles(v, nt):
    return np.ascontiguousarray(np.asarray(v, np.float32).reshape(nt, 128).T)


def _consts():
    c = np.zeros((128, 5, 128), np.float32)
    idx = np.arange(128)
    same = (idx[:, None] // 64) == (idx[None, :] // 64)
    c[:, 0, :] = np.eye(128)
    c[:, 1, :] = same & (idx[None, :] < idx[:, None])
    c[:, 2, :] = same & (idx[None, :] > idx[:, None])
    c[:, 3, :] = same & (idx[None, :] >= idx[:, None])
    c[:, 4, :] = same
    return c


def prep_inputs(inp):
    f = lambda n: np.asarray(inp[n], np.float32)
    x = f("x")
    shared = {
        "gvec": np.concatenate([_tiles(f(n)[0], 8) for n in ("mix_pre_g", "mix_post_g", "ffn_pre_g", "ffn_post_g")], 1),
        "w_in": np.ascontiguousarray(f("w_in")[0]),
        "conv_w": np.ascontiguousarray(f("conv_dw_w")[0].T.reshape(4, 128, 31).transpose(1, 0, 2)),
        "conv_v": np.concatenate([_tiles(f(n)[0], 4) for n in ("conv_dw_b", "conv_ln_g", "conv_ln_b")], 1),
        "mu": _tiles(np.concatenate([f("rwkv_mu")[0], np.zeros(15 * 128 - NRW, np.float32)]), 15),
        "rwv": np.concatenate([_tiles(f(n)[0].reshape(-1), 4) for n in ("w0", "a0", "k_k", "k_a", "lnx_g", "lnx_b", "r_k")], 1),
        "w2a2": np.ascontiguousarray(np.concatenate([f("w2")[0], f("a2")[0]], 0)),
        "g2": np.ascontiguousarray(f("g2")[0]),
        "w_out": np.ascontiguousarray(f("w_out")[0]),
        "w_up": np.ascontiguousarray(f("w_up")[0]),
        "ffn_w": np.ascontiguousarray(f("ffn_dw_w")[0].T.reshape(44, 128, 3).transpose(1, 0, 2)),
        "ffn_b": _tiles(f("ffn_dw_b")[0], 44),
        "w_down": np.ascontiguousarray(f("w_down")[0]),
        "consts": _consts(),
    }
    maps = []
    for core in range(NCORE):
        b, q = divmod(core, 4)
        xt = np.zeros((D, NX), np.float32)
        lo = q * TOK - (HALO + 1)
        if q == 0:
            xt[:, HALO + 1:] = x[b, 0:TOK].T
        else:
            xt[:, :] = x[b, lo:lo + NX].T
        m = dict(shared)
        m["xT"] = xt
        m["ada_w"] = np.ascontiguousarray(f("ada_w")[0][:, q * 1536:(q + 1) * 1536])
        m["ada_b"] = _tiles(f("ada_b")[0][q * 1536:(q + 1) * 1536], 12)
        m["cvec"] = _tiles(f("c")[b], 8)
        m["hv"] = np.full((128, 1), 0.0 if q == 0 else 1.0, np.float32)
        cm = np.zeros((128, 4), np.float32)
        cm[:, :q] = 1.0
        m["cmask"] = cm
        maps.append(m)
    return maps


_CACHE = {}


def kernel(**inputs):
    if "prog" not in _CACHE:
        _CACHE["prog"] = build_program()
    P = _CACHE["prog"]
    maps = prep_inputs(inputs)
    res = run_bass_kernel_spmd(P.nc, maps, core_ids=list(range(NCORE)))
    out = np.zeros((2, SEQ, D), np.float32)
    for core in range(NCORE):
        b, q = divmod(core, 4)
        out[b, q * TOK:(q + 1) * TOK, :] = np.asarray(res.results[core]["outT"]).T
    return out
```

```python
import contextlib
import numpy as np
import concourse.bass as bass
import concourse.mybir as mybir
from concourse.bass_utils import run_bass_kernel_spmd

F32 = mybir.dt.float32
BF16 = mybir.dt.bfloat16
ALU = mybir.AluOpType
AF = mybir.ActivationFunctionType

D = 1024
SEQ = 8192
NCORE = 8
TOK = 2048
HALO = 128
NT = TOK + HALO
NX = NT + 1
NRW = 1824
DFF = 2816
RMS_EPS = 1e-6
LN_EPS = 1e-5
LNX_EPS = 64e-5
DECAY_C = float(np.exp(-0.5))


class Buf:
    __slots__ = ("name", "lw", "rd", "excl")

    def __init__(self, name):
        self.name = name
        self.lw = None
        self.rd = []
        self.excl = False


class Sched:
    ENG = ("pe", "act", "dve", "pool", "sp")

    def __init__(self, nc):
        self.nc = nc
        self.ops = {e: [] for e in self.ENG}
        self.cnt = {e: 0 for e in self.ENG}
        self.pos = {e: 0 for e in self.ENG}
        self.seen = {e: {} for e in self.ENG}
        self.pending = {e: [] for e in self.ENG}
        self.sems = {}
        self.semval = {}
        self.epoch = {e: 0 for e in self.ENG}
        for e in self.ENG:
            self.sems[e + "#0"] = nc.alloc_semaphore(name="s_" + e + "_0")
        self.EPOCH_MAX = 8000
        self.nsem = 0

    def new_sem(self, name):
        k = "d_" + name
        self.sems[k] = self.nc.alloc_semaphore(name=k)
        self.semval[k] = 0
        return k

    def _need(self, eng, ev, waits, same_ok_dist=None):
        if ev is None:
            return
        semkey, val, weng, wpos = ev
        if weng == eng and semkey.startswith(eng + "#"):
            if eng == "pe":
                return
            if same_ok_dist is None:
                return
            if self.pos[eng] - wpos > same_ok_dist:
                return
        if self.seen[eng].get(semkey, 0) >= val:
            return
        waits[semkey] = max(waits.get(semkey, 0), val)

    def op(self, eng, fn, reads=(), writes=(), signal=True):
        waits = {}
        for b in reads:
            self._need(eng, b.lw, waits, same_ok_dist=8)
            if b.excl:
                for ev in b.rd:
                    self._need(eng, ev, waits, same_ok_dist=None)
        for b in writes:
            self._need(eng, b.lw, waits, same_ok_dist=8)
            for ev in b.rd:
                self._need(eng, ev, waits, same_ok_dist=None)
        for k, v in waits.items():
            self.seen[eng][k] = v
        wl = list(waits.items())
        if not signal:
            self.ops[eng].append((wl, fn, None, 0))
            self.pos[eng] += 1
            self.pending[eng].append((list(reads), list(writes)))
            return None
        if self.cnt[eng] >= self.EPOCH_MAX:
            self.epoch[eng] += 1
            self.cnt[eng] = 0
            nk = eng + "#%d" % self.epoch[eng]
            self.sems[nk] = self.nc.alloc_semaphore(name="s_%s_%d" % (eng, self.epoch[eng]))
        sk = eng + "#%d" % self.epoch[eng]
        self.cnt[eng] += 1
        ev = (sk, self.cnt[eng], eng, self.pos[eng])
        self.ops[eng].append((wl, fn, sk, 1))
        self.pos[eng] += 1
        groups = self.pending[eng] + [(list(reads), list(writes))]
        self.pending[eng] = []
        for rds, wrs in groups:
            for b in rds:
                b.rd.append(ev)
                if len(b.rd) > 32:
                    b.rd = b.rd[-32:]
            for b in wrs:
                b.lw = ev
                b.rd = []
        return ev

    def dma(self, queue, semkey, fn, reads=(), writes=()):
        waits = {}
        big = 1 << 30
        for b in reads:
            self._need(queue, b.lw, waits, same_ok_dist=big)
        for b in writes:
            self._need(queue, b.lw, waits, same_ok_dist=big)
            for ev in b.rd:
                self._need(queue, ev, waits, same_ok_dist=big)
        for k, v in waits.items():
            self.seen[queue][k] = v
        self.semval[semkey] += 16
        ev = (semkey, self.semval[semkey], "dma", 0)
        self.ops[queue].append((list(waits.items()), fn, semkey, 16))
        self.pos[queue] += 1
        for b in reads:
            b.rd.append(ev)
        for b in writes:
            b.lw = ev
            b.rd = []
        return ev

    def raw(self, eng, waits, fn, semkey, inc):
        self.ops[eng].append((list(waits), fn, semkey, inc))
        self.pos[eng] += 1

    def wait_all(self, eng, bufs):
        waits = {}
        big = 1 << 30
        for b in bufs:
            self._need(eng, b.lw, waits, same_ok_dist=big)
            for ev in b.rd:
                self._need(eng, ev, waits, same_ok_dist=big)
        for k, v in waits.items():
            self.seen[eng][k] = v
        self.ops[eng].append((list(waits.items()), None, None, 0))

    def emit(self):
        nc = self.nc
        sems = self.sems
        for e in self.ENG:
            assert not self.pending[e], "unsignalled tail on " + e

        def run(engobj, lst):
            for wl, fn, sk, inc in lst:
                for k, v in wl:
                    engobj.wait_ge(sems[k], v)
                if fn is None:
                    continue
                ins = fn(engobj)
                if sk is not None:
                    ins.then_inc(sems[sk], inc)

        with nc.Block() as block:
            @block.tensor
            def _(e):
                run(e, self.ops["pe"])

            @block.scalar
            def _(e):
                run(e, self.ops["act"])

            @block.vector
            def _(e):
                run(e, self.ops["dve"])

            @block.gpsimd
            def _(e):
                run(e, self.ops["pool"])

            @block.sync
            def _(e):
                run(e, self.ops["sp"])
        self.ops = {e: [] for e in self.ENG}


class StopBuild(Exception):
    pass


class T:
    def __init__(self, ap_tensor, name):
        self.t = ap_tensor
        self.b = Buf(name)
        self.sem = None

    def __getitem__(self, idx):
        return self.t[idx]


class Prog:
    def __init__(self, debug=None, phases=("0", "A", "X", "B", "C")):
        self.debug = debug or {}
        self.phases = phases
        self.nc = bass.Bass("TRN2", target_bir_lowering=False)
        self.S = Sched(self.nc)
        self.dram = {}
        self.dbuf = {}
        self.nbank = 0

    def din(self, name, shape, dt=F32):
        t = self.nc.dram_tensor(name, list(shape), dt, kind="ExternalInput")
        self.dram[name] = t
        self.dbuf[name] = Buf(name)
        return t.ap()

    def dout(self, name, shape, dt=F32):
        t = self.nc.dram_tensor(name, list(shape), dt, kind="ExternalOutput")
        self.dram[name] = t
        self.dbuf[name] = Buf(name)
        return t.ap()

    def dint(self, name, shape, dt=F32):
        t = self.nc.dram_tensor(name, list(shape), dt)
        self.dram[name] = t
        self.dbuf[name] = Buf(name)
        return t.ap()

    def sb(self, stack, name, shape, dt=F32):
        t = stack.enter_context(self.nc.sbuf_tensor("sb_" + name, list(shape), dt))
        return T(t, name)

    def ps(self, stack, name, shape=(128, 512), dt=F32):
        t = stack.enter_context(self.nc.psum_tensor("ps_" + name, list(shape), dt))
        return T(t, name)

    def mm(self, bank, out, lhsT, rhs, start, stop, rd, signal=None):
        sig = stop if signal is None else signal
        self.S.op("pe", lambda e, o=out, l=lhsT, r=rhs, st=start, sp=stop: e.matmul(
            o, l, r, start=st, stop=sp, skip_group_check=True),
            reads=[x.b for x in rd], writes=[bank.b], signal=sig)

    def act(self, out, in_, func, wr, rd, bias=None, scale=None):
        kw = {}
        if func == AF.Copy and scale is not None and not isinstance(scale, float):
            func = AF.Identity
        if bias is not None:
            kw["bias"] = bias
        if scale is not None:
            kw["scale"] = scale
        self.S.op("act", lambda e, o=out, i=in_, f=func, kw=kw: e.activation(out=o, in_=i, func=f, **kw),
                  reads=[x.b for x in rd], writes=[x.b for x in wr])

    def tt(self, eng, out, in0, in1, op, wr, rd):
        self.S.op(eng, lambda e, o=out, a=in0, b=in1, p=op: e.tensor_tensor(out=o, in0=a, in1=b, op=p),
                  reads=[x.b for x in rd], writes=[x.b for x in wr])

    def ts(self, eng, out, in0, s1, s2, op0, op1, wr, rd):
        if op1 is None:
            self.S.op(eng, lambda e, o=out, a=in0, s=s1, p=op0: e.tensor_scalar(
                out=o, in0=a, scalar1=s, scalar2=None, op0=p),
                reads=[x.b for x in rd], writes=[x.b for x in wr])
        else:
            self.S.op(eng, lambda e, o=out, a=in0, s=s1, s_=s2, p=op0, p_=op1: e.tensor_scalar(
                out=o, in0=a, scalar1=s, scalar2=s_, op0=p, op1=p_),
                reads=[x.b for x in rd], writes=[x.b for x in wr])

    def stt(self, out, in0, scalar, in1, op0, op1, wr, rd):
        self.S.op("dve", lambda e, o=out, a=in0, s=scalar, b=in1, p=op0, p_=op1: e.scalar_tensor_tensor(
            out=o, in0=a, scalar=s, in1=b, op0=p, op1=p_),
            reads=[x.b for x in rd], writes=[x.b for x in wr])

    def rsqrt(self, out, in_, eps, scale, wr_tile, rd):
        self.act(out, in_, AF.Ln, [wr_tile], rd, bias=float(eps), scale=float(scale))
        self.act(out, out, AF.Exp, [wr_tile], [wr_tile], scale=-0.5)

    def copy(self, eng, out, in_, wr, rd):
        if eng == "act":
            self.act(out, in_, AF.Copy, wr, rd)
        else:
            self.S.op(eng, lambda e, o=out, i=in_: e.tensor_copy(out=o, in_=i),
                      reads=[x.b for x in rd], writes=[x.b for x in wr])

    def memset(self, eng, ap, val, wr):
        self.S.op(eng, lambda e, a=ap, v=val: e.memset(a, v), reads=[], writes=[x.b for x in wr])

    def dma(self, queue, sem, out, in_, wr=(), rd=(), dwr=(), drd=()):
        owner = (list(wr) + list(rd))[0]
        if getattr(owner, "sem", None) is None:
            owner.sem = self.S.new_sem("t%d" % len(self.S.sems))
        sem = owner.sem
        self.S.dma(queue, sem, lambda e, o=out, i=in_: e.dma_start(out=o, in_=i),
                   reads=[x.b for x in rd] + [self.dbuf[n] for n in drd],
                   writes=[x.b for x in wr] + [self.dbuf[n] for n in dwr])

    def stop(self, level):
        if self.debug.get("stop") == level:
            raise StopBuild()

    def dbg(self, name, tile_ap, shape, rd, dt=F32):
        if name not in self.debug:
            return
        o = self.dout("dbg_" + name, shape, dt)
        self.dma("sp", None, o, tile_ap, rd=rd, dwr=["dbg_" + name])
        self.S.wait_all("sp", [self.dbuf["dbg_" + name]])


def build_program(debug=None, phases=("0", "A", "X", "B", "C")):
    P = Prog(debug, phases)
    nc, S = P.nc, P.S

    xT = P.din("xT", [D, NX])
    cvec = P.din("cvec", [128, 8])
    ada_w = P.din("ada_w", [D, 1536])
    ada_b = P.din("ada_b", [128, 12])
    gvec = P.din("gvec", [128, 32])
    w_in = P.din("w_in", [D, 2848])
    cw = P.din("conv_w", [128, 4, 31])
    cvecs = P.din("conv_v", [128, 12])
    mu = P.din("mu", [128, 15])
    rwv = P.din("rwv", [128, 28])
    w2a2 = P.din("w2a2", [128, 512])
    g2 = P.din("g2", [160, 512])
    w_out = P.din("w_out", [D, D])
    w_up = P.din("w_up", [D, 2 * DFF])
    fw = P.din("ffn_w", [128, 44, 3])
    fb = P.din("ffn_b", [128, 44])
    w_down = P.din("w_down", [DFF, D])
    hvd = P.din("hv", [128, 1])
    cmaskd = P.din("cmask", [128, 4])
    consts = P.din("consts", [128, 5, 128])
    outT = P.dout("outT", [D, TOK])

    cc2_in = P.dint("cc2_in", [128, 12])
    cc2_out = P.dint("cc2_out", [4 * 128, 12])
    yq_s = P.dint("yq_s", [4, 128, NT], BF16)
    rp_s = P.dint("rp_s", [4, 128, NT], BF16)
    g_s = P.dint("g_s", [4, 128, NT], BF16)
    bon_s = P.dint("bon_s", [4, 128, NT], BF16)
    cc_in = P.dint("cc_in", [128, 1024])
    cc_out = P.dint("cc_out", [4 * 128, 1024])
    xm_s = P.dint("xm_s", [D, TOK + 2])

    blocks = [(0, 128)] + [(128 + 512 * i, 512) for i in range(4)]

    with contextlib.ExitStack() as gs:
        cst = P.sb(gs, "cst", [128, 5, 128])
        identb = P.sb(gs, "identb", [128, 128], BF16)
        onesb = P.sb(gs, "onesb", [128, 128], BF16)
        blkb = P.sb(gs, "blkb", [128, 128], BF16)
        modv = P.sb(gs, "modv", [128, 48])
        gv = P.sb(gs, "gv", [128, 32])
        Am = P.sb(gs, "Am", [128, 8])
        Gm = P.sb(gs, "Gm", [128, 8])
        Af = P.sb(gs, "Af", [128, 8])
        Gf = P.sb(gs, "Gf", [128, 8])
        hv = P.sb(gs, "hv", [128, 1])
        cmask = P.sb(gs, "cmask", [128, 4])
        zst = P.sb(gs, "zst", [128, 4, 128], BF16)
        banks = [P.ps(gs, "bank%d" % i) for i in range(8)]
        for b_ in banks:
            b_.b.excl = True
        sem_c = sem_x = sem_w = sem_o = sem_o2 = None

        held = set()

        def nextbank(hold=False):
            while True:
                b = banks[P.nbank % 8]
                P.nbank += 1
                if id(b) not in held:
                    break
            if hold:
                held.add(id(b))
            return b

        def release(b):
            held.discard(id(b))

        with contextlib.ExitStack() as st:
            csb = P.sb(st, "csb", [128, 8])
            scv = P.sb(st, "scv", [128, 8])
            adab = P.sb(st, "adab", [128, 12])
            wbuf = [P.sb(st, "adaw%d" % i, [128, 1536]) for i in range(3)]
            modrow = P.sb(st, "modrow", [1, 1536])
            modq = P.sb(st, "modq", [128, 12])
            P.dma("sp", sem_c, cst[:], consts, wr=[cst])
            P.dma("sp", sem_c, csb[:], cvec, wr=[csb])
            P.dma("sp", sem_c, adab[:], ada_b, wr=[adab])
            P.dma("sp", sem_c, gv[:], gvec, wr=[gv])
            P.dma("sp", sem_c, hv[:], hvd, wr=[hv])
            P.dma("sp", sem_c, cmask[:], cmaskd, wr=[cmask])
            P.copy("dve", identb[:], cst[:, 0, :], [identb], [cst])
            P.copy("dve", blkb[:], cst[:, 4, :], [blkb], [cst])
            P.memset("dve", onesb[:], 1.0, [onesb])
            P.act(scv[:], csb[:], AF.Silu, [scv], [csb])
            bk = [nextbank() for _ in range(3)]
            for kt in range(8):
                wb = wbuf[kt % 3]
                P.dma("sp" if kt % 2 else "pool", sem_w, wb[:], ada_w[kt * 128:(kt + 1) * 128, :], wr=[wb])
                for j in range(3):
                    P.mm(bk[j], bk[j][0:1, :], scv[:, kt:kt + 1], wb[:, j * 512:(j + 1) * 512],
                         kt == 0, kt == 7, [scv, wb], signal=True if j == 2 else None)
            for j in range(3):
                P.copy("act" if j % 2 else "dve", modrow[0:1, j * 512:(j + 1) * 512], bk[j][0:1, :], [modrow], [bk[j]])
            one1 = P.sb(st, "one1", [1, 1])
            P.memset("dve", one1[:], 1.0, [one1])
            bkt = nextbank()
            for t in range(12):
                P.mm(bkt, bkt[:, t:t + 1], modrow[0:1, t * 128:(t + 1) * 128], one1[0:1, 0:1], t == 0, True, [modrow, one1],
                     signal=(t == 11))
            P.tt("dve", modq[:], bkt[:, 0:12], adab[:], ALU.add, [modq], [bkt, adab])
            P.dma("pool", None, cc2_in, modq[:], rd=[modq], dwr=["cc2_in"])
            S.sems["cc2"] = nc.alloc_semaphore(name="cc2_sem")
            ev_ = P.dbuf["cc2_in"].lw
            S.raw("pool", [(ev_[0], ev_[1])], lambda e: e.collective_compute(
                "AllGather", ALU.bypass, replica_groups=[[0, 1, 2, 3], [4, 5, 6, 7]],
                ins=[P.dram["cc2_in"].ap().opt()], outs=[P.dram["cc2_out"].ap().opt()]), "cc2", 1)
            P.dbuf["cc2_out"].lw = ("cc2", 1, "dma", 0)
            P.dma("pool", None, modv[:].rearrange("p (r j) -> p r j", j=12), cc2_out.rearrange("(r p) j -> p r j", p=128),
                  wr=[modv], drd=["cc2_out"])
            for (dst, gcol, sccol, gate) in ((Am, 0, 8, False), (Gm, 8, 16, True), (Af, 16, 32, False), (Gf, 24, 40, True)):
                if gate:
                    P.stt(dst[:], gv[:, gcol:gcol + 8], 32.0, modv[:, sccol:sccol + 8], ALU.mult, ALU.mult, [dst], [gv, modv])
                else:
                    P.ts("dve", dst[:], modv[:, sccol:sccol + 8], 1.0, None, ALU.add, None, [dst], [modv])
                    P.stt(dst[:], gv[:, gcol:gcol + 8], 32.0, dst[:], ALU.mult, ALU.mult, [dst], [gv, dst])
            P.dbg("modv", modv[:], [128, 48], [modv])
            P.dbg("Am", Am[:], [128, 8], [Am])
            S.emit()

        def load_x(xb, c0, ncol):
            P.dma("sp", sem_x, xb[:, :, 0:ncol], xT.rearrange("(kt p) t -> p kt t", p=128)[:, :, c0:c0 + ncol], wr=[xb])

        def load_x_and_norm(xb, sq, rstd, tmp, hb, c0, ncol, Avec, shcol, do_load=True):
            if do_load:
                load_x(xb, c0, ncol)
            bk = nextbank()
            for kt in range(8):
                s = sq[kt % 2]
                P.act(s[:, 0:ncol], xb[:, kt, 0:ncol], AF.Square, [s], [xb])
                P.mm(bk, bk[:, 0:ncol], onesb[:], s[:, 0:ncol], kt == 0, kt == 7, [onesb, s], signal=True)
            P.rsqrt(rstd[:, 0:ncol], bk[:, 0:ncol], float(D * RMS_EPS), 1.0, rstd, [bk])
            for kt in range(8):
                t = tmp[kt % 2]
                P.tt("dve", t[:, 0:ncol], xb[:, kt, 0:ncol], rstd[:, 0:ncol], ALU.mult, [t], [xb, rstd])
                P.act(hb[:, kt, 0:ncol], t[:, 0:ncol], AF.Identity, [hb], [t, Avec, modv],
                      bias=modv[:, shcol + kt:shcol + kt + 1], scale=Avec[:, kt:kt + 1])

        if "A" in phases:
            with contextlib.ExitStack() as st:
                wrw = P.sb(st, "wrw", [128, 8, NRW], BF16)
                lw = P.sb(st, "lw", [128, 512], BF16)
                g2a = P.sb(st, "g2a", [128, 512], BF16)
                g2b = P.sb(st, "g2b", [32, 512], BF16)
                muv = P.sb(st, "muv", [128, 15])
                rv = P.sb(st, "rv", [128, 28])
                omka = P.sb(st, "omka", [128, 4])
                rkb = P.sb(st, "rkb", [128, 4, 128], BF16)
                eab = P.sb(st, "eab", [128, 256], BF16)
                msk = P.sb(st, "msk", [128, 2, 512], BF16)
                ii2 = P.sb(st, "ii2", [128, 256], BF16)
                ii4 = P.sb(st, "ii4", [128, 512], BF16)
                rmask = P.sb(st, "rmask", [128, 512])
                xb = P.sb(st, "xb", [128, 8, 513])
                rstd = P.sb(st, "rstd", [128, 513])
                hbs = [P.sb(st, "hb%d" % i, [128, 8, 513], BF16) for i in range(2)]
                stage = [P.sb(st, "stage%d" % i, [128, 513]) for i in range(2)]
                carry = P.sb(st, "carry", [128, 15])
                dsh = [P.sb(st, "dsh%d" % i, [128, 512]) for i in range(1)]
                pl = P.sb(st, "pl", [128, 3, 512])
                pp = [P.sb(st, "pp%d" % i, [128, 3, 512]) for i in range(2)]
                twb = P.sb(st, "twb", [128, 512], BF16)
                sgd = P.sb(st, "sgd", [128, 2, 512], BF16)
                sc = [P.sb(st, "sc%d" % i, [128, 513]) for i in range(9)]
                tmp = [sc[0], sc[1]]
                scb = [P.sb(st, "scb%d" % i, [128, 513], BF16) for i in range(2)]
                sq = scb
                Wend = P.sb(st, "Wend", [128, 4, 8])
                RhA = P.sb(st, "RhA", [128, 4, 512], BF16)
                KKA = P.sb(st, "KKA", [128, 4, 512], BF16)
                BhA = P.sb(st, "BhA", [128, 4, 512], BF16)
                KhA = P.sb(st, "KhA", [128, 4, 512], BF16)
                VbA = P.sb(st, "VbA", [128, 4, 512], BF16)
                gblk = P.sb(st, "gblk", [128, 4, 512], BF16)
                bonblk = P.sb(st, "bonblk", [128, 4, 512], BF16)
                yqblk = P.sb(st, "yqblk", [128, 4, 512], BF16)
                rpblk = P.sb(st, "rpblk", [128, 4, 512], BF16)
                TM = [P.sb(st, "TM%d" % i, [128, 1024], BF16) for i in range(4)]
                AMh = [[P.sb(st, "AMh%d_%d" % (i, sd), [128, 512], BF16) for sd in range(2)] for i in range(4)]
                AK = [P.sb(st, "AK%d" % sd, [128, 4, 128], BF16) for sd in range(2)]
                AVs = P.sb(st, "AVs", [128, 4, 128], BF16)
                PQ = [[P.sb(st, "PQ%d_%d" % (l, i), [128, 512], BF16) for i in range(4)] for l in range(2)]
                GG = [[P.sb(st, "GG%d_%d" % (l, i), [128, 512], BF16) for i in range(2)] for l in range(2)]
                KU = [P.sb(st, "KU%d" % i, [128, 512], BF16) for i in range(4)]
                RT = P.sb(st, "RT", [128, 4, 128], BF16)
                MT = [P.sb(st, "MT%d" % i, [128, 4, 128], BF16) for i in range(2)]
                N0 = [P.sb(st, "N0%d" % i, [128, 4, 128]) for i in range(2)]
                ZQ = [P.sb(st, "ZQ%d" % i, [128, 4, 128], BF16) for i in range(2)]
                ZP = [P.sb(st, "ZP%d" % i, [128, 4, 128], BF16) for i in range(2)]
                ZQv = [[T(ZQ[i].t[:, pr, :], "ZQ%d_%d" % (i, pr)) for pr in range(4)] for i in range(2)]
                ZPv = [[T(ZP[i].t[:, pr, :], "ZP%d_%d" % (i, pr)) for pr in range(4)] for i in range(2)]

                xflat = xb.t[:].rearrange("p k c -> p (k c)")
                wstA = []
                for i in range(2):
                    t_ = T(xflat[:, i * NRW:(i + 1) * NRW], "wstA%d" % i)
                    wstA.append(t_)
                for kt in range(8):
                    stg_ = wstA[kt % 2]
                    P.dma("sp", sem_w, stg_[:], w_in[kt * 128:(kt + 1) * 128, 1024:2848], wr=[stg_])
                    P.copy("act" if kt % 2 else "dve", wrw[:, kt, :], stg_[:], [wrw], [stg_])
                xb.b.rd = list(wstA[0].b.rd) + list(wstA[1].b.rd)
                P.dma("pool", sem_w, lw[:], w2a2, wr=[lw])
                P.dma("pool", sem_w, g2a[:], g2[0:128, :], wr=[g2a])
                P.dma("pool", sem_w, g2b[:], g2[128:160, :], wr=[g2b])
                P.dma("sp", sem_c, muv[:], mu, wr=[muv])
                P.dma("sp", sem_c, rv[:], rwv, wr=[rv])
                P.ts("dve", omka[:], rv[:, 12:16], -1.0, 1.0, ALU.mult, ALU.add, [omka], [rv])
                for pr in range(4):
                    P.ts("dve", rkb[:, pr, :], cst[:, 4, :], rv[:, 24 + pr:25 + pr], None, ALU.mult, None, [rkb], [cst, rv])
                P.memset("dve", eab[:], 0.0, [eab])
                P.copy("dve", eab[0:64, 0:128], cst[0:64, 0, :], [eab], [cst])
                P.copy("dve", eab[64:128, 128:256], cst[64:128, 0, :], [eab], [cst])
                P.ts("dve", msk[:, 0, 0:128], cst[:, 1, :], -1.0, None, ALU.mult, None, [msk], [cst])
                P.ts("dve", msk[:, 0, 128:256], cst[:, 2, :], -1.0, None, ALU.mult, None, [msk], [cst])
                P.copy("dve", msk[:, 0, 256:384], cst[:, 2, :], [msk], [cst])
                P.copy("dve", msk[:, 0, 384:512], cst[:, 3, :], [msk], [cst])
                for j in range(4):
                    P.copy("dve", msk[:, 1, j * 128:(j + 1) * 128], cst[:, 3, :], [msk], [cst])
                    P.copy("dve", ii4[:, j * 128:(j + 1) * 128], cst[:, 0, :], [ii4], [cst])
                for j in range(2):
                    P.copy("dve", ii2[:, j * 128:(j + 1) * 128], cst[:, 0, :], [ii2], [cst])
                P.memset("dve", rmask[:], 1.0, [rmask])
                P.memset("dve", rmask[:].rearrange("p (c t) -> p c t", t=64)[:, :, 0:1], 0.0, [rmask])
                P.memset("dve", carry[:], 0.0, [carry])
                for i in range(4):
                    P.memset("pool", KU[i][:], 0.0, [KU[i]])
                for i in range(2):
                    P.memset("pool", ZQ[i][:], 0.0, [ZQ[i]])
                    P.memset("pool", ZP[i][:], 0.0, [ZP[i]])
                    P.memset("pool", MT[i][:], 0.0, [MT[i]])
                for pr in range(4):
                    P.copy("dve", ZP[0][:, pr, :], cst[:, 0, :], [ZP[0]], [cst])
                zcur = 0
                ngroup = 0

                try:
                  P.stop(1)
                  for bi, (t0, n) in enumerate(blocks):
                      ncol = n + 1 if bi == 0 else n
                      c0 = 0 if bi == 0 else t0 + 1
                      hb = hbs[bi % 2]
                      if bi == 0:
                          load_x_and_norm(xb, sq, rstd, tmp, hb, c0, ncol, Am, 0, do_load=True)
                          load_x(xb, blocks[1][0] + 1, blocks[1][1])

                      def proj_shift(j, dst, M=128):
                          bk = nextbank()
                          stg = stage[j % 2]
                          dd = dsh[0]
                          for kt in range(8):
                              P.mm(bk, bk[0:M, 0:ncol], wrw[:, kt, j * 128:j * 128 + M], hb[:, kt, 0:ncol],
                                   kt == 0, kt == 7, [wrw, hb])
                          if bi == 0:
                              P.act(stg[0:M, 0:ncol], bk[0:M, 0:ncol], AF.Copy, [stg], [bk, hv], scale=hv[0:M, 0:1])
                          else:
                              P.copy("pool", stg[0:M, 0:1], carry[0:M, j:j + 1], [stg], [carry])
                              P.act(stg[0:M, 1:n + 1], bk[0:M, 0:n], AF.Copy, [stg], [bk])
                          P.tt("dve", dd[0:M, 0:n], stg[0:M, 0:n], stg[0:M, 1:n + 1], ALU.subtract, [dd], [stg])
                          P.stt(dst, dd[0:M, 0:n], muv[0:M, j:j + 1], stg[0:M, 1:n + 1], ALU.mult, ALU.add, [dstT[0]], [dd, muv, stg])
                          P.copy("pool", carry[0:M, j:j + 1], stg[0:M, n:n + 1], [carry], [stg])

                      dstT = [pl]
                      proj_shift(12, pl[:, 0, 0:n])
                      proj_shift(13, pl[:, 1, 0:n])
                      proj_shift(14, pl[0:32, 2, 0:n], M=32)
                      P.act(twb[0:64, 0:n], pl[0:64, 0, 0:n], AF.Tanh, [twb], [pl])
                      P.copy("dve", twb[64:128, 0:n], pl[64:128, 0, 0:n], [twb], [pl])
                      P.act(sgd[:, 0, 0:n], pl[:, 1, 0:n], AF.Sigmoid, [sgd], [pl])
                      P.act(sgd[0:32, 1, 0:n], pl[0:32, 2, 0:n], AF.Sigmoid, [sgd], [pl])

                      def proj_pair(pq):
                          ppq = pp[pq % 2]
                          dstT[0] = ppq
                          proj_shift(pq, ppq[:, 0, 0:n])
                          proj_shift(4 + pq, ppq[:, 1, 0:n])
                          proj_shift(8 + pq, ppq[:, 2, 0:n])

                      proj_pair(0)
                      for pr in range(4):
                          ppt = pp[pr % 2]
                          if pr + 1 < 4:
                              proj_pair(pr + 1)
                          r_ = ppt[:, 0, 0:n]
                          k_ = ppt[:, 1, 0:n]
                          v_ = ppt[:, 2, 0:n]
                          cs_ = slice(pr * 128, (pr + 1) * 128)
                          bkw = nextbank()
                          P.mm(bkw, bkw[:, 0:n], lw[0:64, cs_], twb[0:64, 0:n], True, True, [lw, twb])
                          sg_ = sc[0]
                          P.act(sg_[:, 0:n], bkw[:, 0:n], AF.Sigmoid, [sg_], [bkw, rv], bias=rv[:, pr:pr + 1])
                          bka = nextbank()
                          P.mm(bka, bka[:, 0:n], lw[64:128, cs_], twb[64:128, 0:n], True, True, [lw, twb])
                          a_ = sc[1]
                          P.act(a_[:, 0:n], bka[:, 0:n], AF.Sigmoid, [a_], [bka, rv], bias=rv[:, 4 + pr:5 + pr])
                          bkg = nextbank()
                          P.mm(bkg, bkg[:, 0:n], g2a[:, cs_], sgd[:, 0, 0:n], True, False, [g2a, sgd])
                          P.mm(bkg, bkg[:, 0:n], g2b[:, cs_], sgd[0:32, 1, 0:n], False, True, [g2b, sgd])
                          P.copy("act", gblk[:, pr, 0:n], bkg[:, 0:n], [gblk], [bkg])
                          P.act(scb[0][:, 0:n], k_, AF.Square, [scb[0]], [ppt, rv], scale=rv[:, 8 + pr:9 + pr])
                          bkn = nextbank()
                          P.mm(bkn, bkn[:, 0:n], blkb[:], scb[0][:, 0:n], True, True, [blkb, scb[0]])
                          rn_ = sc[2]
                          P.rsqrt(rn_[:, 0:n], bkn[:, 0:n], 1e-18, 1.0, rn_, [bkn])
                          kk_ = sc[3]
                          P.stt(kk_[:, 0:n], k_, rv[:, 8 + pr:9 + pr], rn_[:, 0:n], ALU.mult, ALU.mult, [kk_], [ppt, rv, rn_])
                          t1_ = sc[4]
                          P.ts("dve", t1_[:, 0:n], a_[:, 0:n], rv[:, 12 + pr:13 + pr], omka[:, pr:pr + 1], ALU.mult, ALU.add,
                               [t1_], [a_, rv, omka])
                          k2_ = sc[5]
                          P.tt("dve", k2_[:, 0:n], k_, t1_[:, 0:n], ALU.mult, [k2_], [ppt, t1_])
                          b_ = sc[4]
                          P.tt("pool", b_[:, 0:n], a_[:, 0:n], kk_[:, 0:n], ALU.mult, [b_], [a_, kk_])
                          P.tt("dve", scb[1][:, 0:n], r_, k2_[:, 0:n], ALU.mult, [scb[1]], [ppt, k2_])
                          bkb = nextbank()
                          P.mm(bkb, bkb[:, 0:n], rkb[:, pr, :], scb[1][:, 0:n], True, True, [rkb, scb[1]])
                          P.tt("dve", bonblk[:, pr, 0:n], bkb[:, 0:n], v_, ALU.mult, [bonblk], [bkb, ppt])
                          P.act(bonblk[:, pr, 0:n], bonblk[:, pr, 0:n], AF.Identity, [bonblk], [bonblk, rv], bias=rv[:, 20 + pr:21 + pr])
                          lg_ = sc[6]
                          P.act(lg_[:, 0:n], sg_[:, 0:n], AF.Copy, [lg_], [sg_], scale=-DECAY_C)
                          csm = sc[7]
                          S.op("dve", lambda e, o=csm[:, 0:n], d0=rmask[:, 0:n], d1=lg_[:, 0:n]: e.tensor_tensor_scan(
                              out=o, data0=d0, data1=d1, initial=0.0, op0=ALU.mult, op1=ALU.add),
                              reads=[rmask.b, lg_.b], writes=[csm.b])
                          cse = sc[0]
                          P.tt("pool", cse[:, 0:n], csm[:, 0:n], lg_[:, 0:n], ALU.subtract, [cse], [csm, lg_])
                          wi_ = sc[8]
                          P.act(wi_[:, 0:n], csm[:, 0:n], AF.Exp, [wi_], [csm])
                          P.copy("pool", Wend[:, pr, 0:n // 64].rearrange("p (c o) -> p c o", o=1),
                                 wi_[:, 0:n].rearrange("p (c t) -> p c t", t=64)[:, :, 63:64], [Wend], [wi_])
                          winv = sc[2]
                          P.act(winv[:, 0:n], csm[:, 0:n], AF.Exp, [winv], [csm], scale=-1.0)
                          we_ = sc[6]
                          P.act(we_[:, 0:n], cse[:, 0:n], AF.Exp, [we_], [cse])
                          P.tt("dve", RhA[:, pr, 0:n], r_, wi_[:, 0:n], ALU.mult, [RhA], [ppt, wi_])
                          P.tt("dve", KKA[:, pr, 0:n], kk_[:, 0:n], we_[:, 0:n], ALU.mult, [KKA], [kk_, we_])
                          P.tt("pool", BhA[:, pr, 0:n], b_[:, 0:n], winv[:, 0:n], ALU.mult, [BhA], [b_, winv])
                          P.tt("dve", KhA[:, pr, 0:n], k2_[:, 0:n], winv[:, 0:n], ALU.mult, [KhA], [k2_, winv])
                          P.copy("act", VbA[:, pr, 0:n], v_, [VbA], [ppt])
                          if bi == 1 and pr == 0:
                              P.dbg("a", a_[:, 0:n], [128, 512], [a_])
                              P.dbg("kk", kk_[:, 0:n], [128, 512], [kk_])
                              P.dbg("k2", k2_[:, 0:n], [128, 512], [k2_])
                              P.dbg("r", ppt[:, 0, 0:n], [128, 512], [ppt])
                              P.dbg("cs", csm[:, 0:n], [128, 512], [csm])

                      if bi == 0:
                          P.stop(3)
                      for gi in range(n // 128):
                          if gi == min(1, n // 128 - 1) and bi + 1 < len(blocks):
                              nt0, nn = blocks[bi + 1]
                              load_x_and_norm(xb, sq, rstd, tmp, hbs[(bi + 1) % 2], nt0 + 1, nn, Am, 0, do_load=False)
                              if bi + 2 < len(blocks):
                                  load_x(xb, blocks[bi + 2][0] + 1, blocks[bi + 2][1])
                          o = gi * 128
                          sl = slice(o, o + 128)
                          last_group = (ngroup == NT // 128 - 1)
                          if last_group:
                              ccv = cc_in.rearrange("p (a c) -> p a c", c=256)
                              P.dma("pool", None, ccv[:, :, 0:128], ZQ[zcur][:], rd=[ZQ[zcur]] + ZQv[zcur], dwr=["cc_in"])
                              P.dma("pool", None, ccv[:, :, 128:256], ZP[zcur][:], rd=[ZP[zcur]] + ZPv[zcur], dwr=["cc_in"])
                          for pr in range(4):
                              for half, (q0, q1) in enumerate(((VbA, KKA), (BhA, KhA))):
                                  bk = nextbank()
                                  P.mm(bk, bk[:, 0:256], q0[:, pr, sl], eab[:], True, False, [q0, eab])
                                  P.mm(bk, bk[:, 256:512], q1[:, pr, sl], eab[:], False, True, [q1, eab])
                                  P.copy("act" if pr % 2 else "dve", TM[pr][:, half * 512:(half + 1) * 512], bk[:], [TM[pr]], [bk])
                          P.stop(5)
                          for pr in range(4):
                              for sd in range(2):
                                  ps_ = slice(sd * 64, (sd + 1) * 64)
                                  bk = nextbank()
                                  combos = ((KKA, BhA), (BhA, KKA), (KhA, KKA), (BhA, RhA))
                                  for ci_, (lh, rh) in enumerate(combos):
                                      P.mm(bk, bk[:, ci_ * 128:(ci_ + 1) * 128], lh[ps_, pr, sl], rh[ps_, pr, sl], ci_ == 0, True,
                                           [lh, rh], signal=(ci_ == 3))
                                  P.tt("dve", AMh[pr][sd][:], bk[:], msk[:, 0, :], ALU.mult, [AMh[pr][sd]], [bk, msk])
                          for sd in range(2):
                              ps_ = slice(sd * 64, (sd + 1) * 64)
                              bk = nextbank()
                              for pr in range(4):
                                  P.mm(bk, bk[:, pr * 128:(pr + 1) * 128], KhA[ps_, pr, sl], RhA[ps_, pr, sl], pr == 0, True,
                                       [KhA, RhA], signal=(pr == 3))
                              P.tt("dve", AK[sd][:].rearrange("p a c -> p (a c)"), bk[:], msk[:, 1, :], ALU.mult, [AK[sd]], [bk, msk])
                          P.stop(6)
                          bk = nextbank()
                          for pr in range(4):
                              for sd in range(2):
                                  vcol = 0 if sd == 0 else 192
                                  P.mm(bk, bk[:, pr * 128 + sd * 64: pr * 128 + sd * 64 + 64],
                                       AMh[pr][sd][:, 256:384], TM[pr][:, vcol:vcol + 64],
                                       pr == 0 and sd == 0, True, [AMh[pr][sd], TM[pr]], signal=(pr == 3 and sd == 1))
                          P.copy("act", AVs[:].rearrange("p a c -> p (a c)"), bk[:], [AVs], [bk])
                          P.stop(7)
                          def Pk(lvl, pr, sd):
                              if lvl == 0:
                                  return AMh[pr][sd], AMh[pr][sd][:, 0:128]
                              t_ = PQ[lvl % 2][pr]
                              return t_, t_[:, sd * 256:sd * 256 + 128]

                          def Qk(lvl, pr, sd):
                              if lvl == 0:
                                  return AMh[pr][sd], AMh[pr][sd][:, 128:256]
                              t_ = PQ[lvl % 2][pr]
                              return t_, t_[:, sd * 256 + 128:sd * 256 + 256]

                          for hp in range(2):
                              for j in range(2):
                                  pr = hp * 2 + j
                                  for sd in range(2):
                                      gc = slice(j * 256 + sd * 128, j * 256 + (sd + 1) * 128)
                                      P.tt("pool", GG[0][hp][:, gc], AMh[pr][sd][:, 128:256], ii2[:, 0:128], ALU.add,
                                           [GG[0][hp]], [AMh[pr][sd], ii2])
                          def g_update(lvl):
                              src, dst = (lvl - 1) % 2, lvl % 2
                              for hp in range(2):
                                  bk = nextbank()
                                  fst = True
                                  for j in range(2):
                                      pr = hp * 2 + j
                                      for sd in range(2):
                                          gc = slice(j * 256 + sd * 128, j * 256 + (sd + 1) * 128)
                                          pt, pa = Pk(lvl, pr, sd)
                                          P.mm(bk, bk[:, gc], pa, GG[src][hp][:, gc], fst, True,
                                               [pt, GG[src][hp]], signal=(j == 1 and sd == 1))
                                          fst = False
                                  P.tt("dve", GG[dst][hp][:], bk[:], GG[src][hp][:], ALU.add, [GG[dst][hp]], [bk, GG[src][hp]])

                          for lvl in range(1, 6):
                              src, dst = (lvl - 1) % 2, lvl % 2
                              for pr in range(4):
                                  bk = nextbank()
                                  fst = True
                                  for sd in range(2):
                                      pt, pa = Pk(lvl - 1, pr, sd)
                                      qt, qa = Qk(lvl - 1, pr, sd)
                                      last = (sd == 1)
                                      P.mm(bk, bk[:, sd * 256:sd * 256 + 128], qa, pa, fst, True, [pt, qt],
                                           signal=(last and lvl == 5))
                                      fst = False
                                      if lvl < 5:
                                          P.mm(bk, bk[:, sd * 256 + 128:sd * 256 + 256], pa, qa, False, True, [pt, qt], signal=last)
                                  ev_eng = "act"
                                  if lvl < 5:
                                      P.copy(ev_eng, PQ[dst][pr][:], bk[:], [PQ[dst][pr]], [bk])
                                  else:
                                      P.copy(ev_eng, PQ[dst][pr][:].rearrange("p (a c) -> p a c", c=256)[:, :, 0:128],
                                             bk[:].rearrange("p (a c) -> p a c", c=256)[:, :, 0:128], [PQ[dst][pr]], [bk])
                              if lvl >= 2:
                                  g_update(lvl - 1)
                          g_update(5)
                          P.stop(8)
                          gfin = GG[5 % 2]
                          for hp in range(2):
                              bk = nextbank()
                              fst = True
                              for j in range(2):
                                  pr = hp * 2 + j
                                  for sd in range(2):
                                      gc = slice(j * 256 + sd * 128, j * 256 + (sd + 1) * 128)
                                      kcol = 256 if sd == 0 else 448
                                      oc = j * 256 + sd * 128
                                      P.mm(bk, bk[:, oc:oc + 64], gfin[hp][:, gc], TM[pr][:, kcol:kcol + 64], fst, True,
                                           [gfin[hp], TM[pr]], signal=False)
                                      fst = False
                                      P.mm(bk, bk[:, oc + 64:oc + 128], gfin[hp][:, gc], AVs[:, pr, sd * 64:(sd + 1) * 64], False, True,
                                           [gfin[hp], AVs], signal=(j == 1 and sd == 1))
                              for j in range(2):
                                  pr = hp * 2 + j
                                  src_a = bk[:, j * 256:j * 256 + 128].rearrange("p (a c) -> p a c", c=64)
                                  dst_a = KU[pr][:, 0:256].rearrange("p (a c) -> p a c", c=128)[:, :, 0:64]
                                  src_b = bk[:, j * 256 + 128:j * 256 + 256].rearrange("p (a c) -> p a c", c=64)
                                  dst_b = KU[pr][:, 256:512].rearrange("p (a c) -> p a c", c=128)[:, :, 64:128]
                                  if hp == 0:
                                      P.ts("dve", dst_a, src_a, -1.0, None, ALU.mult, None, [KU[pr]], [bk])
                                      P.ts("dve", dst_b, src_b, -1.0, None, ALU.mult, None, [KU[pr]], [bk])
                                  else:
                                      P.act(dst_a, src_a, AF.Copy, [KU[pr]], [bk], scale=-1.0)
                                      P.act(dst_b, src_b, AF.Copy, [KU[pr]], [bk], scale=-1.0)
                          P.stop(9)
                          bk = nextbank()
                          for pr in range(4):
                              for sd in range(2):
                                  P.mm(bk, bk[:, pr * 128:(pr + 1) * 128], KU[pr][:, sd * 256:sd * 256 + 128],
                                       AMh[pr][sd][:, 384:512], pr == 0 and sd == 0, sd == 1, [KU[pr], AMh[pr][sd]],
                                       signal=(pr == 3 and sd == 1))
                          P.tt("dve", RT[:], bk[:].rearrange("p (a c) -> p a c", c=128), RhA[:, :, sl], ALU.add, [RT], [bk, RhA])
                          P.stop(10)
                          for c in range(2):
                              rows = slice(c * 64, (c + 1) * 64)
                              bk = nextbank()
                              for pr in range(4):
                                  for sd in range(2):
                                      bcol = 512 if sd == 0 else 704
                                      P.mm(bk, bk[:, pr * 128 + sd * 64:pr * 128 + sd * 64 + 64], KU[pr][rows, sd * 256:sd * 256 + 128],
                                           TM[pr][rows, bcol:bcol + 64], pr == 0 and sd == 0, True, [KU[pr], TM[pr]], signal=True)
                              P.tt("dve", MT[c][:].rearrange("p a c -> p (a c)"), bk[:], ii4[:], ALU.add, [MT[c]], [bk, ii4])
                              bk2 = nextbank()
                              fst = True
                              for pr in range(4):
                                  for sd in range(2):
                                      oc = pr * 128 + sd * 64
                                      ucol = 128 if sd == 0 else 448
                                      vcol = 0 if sd == 0 else 192
                                      P.mm(bk2, bk2[:, oc:oc + 64], TM[pr][rows, 512 + sd * 128:512 + (sd + 1) * 128],
                                           KU[pr][rows, ucol:ucol + 64], fst, False, [TM[pr], KU[pr]], signal=False)
                                      fst = False
                                      P.mm(bk2, bk2[:, oc:oc + 64], TM[pr][rows, 768 + sd * 128:768 + (sd + 1) * 128],
                                           TM[pr][rows, vcol:vcol + 64], False, True, [TM[pr]], signal=True)
                              ci = gi * 2 + c
                              for pr in range(4):
                                  if True:
                                      P.act(N0[c][:, pr, :], bk2[:, pr * 128:(pr + 1) * 128], AF.Copy, [N0[c]], [bk2, Wend],
                                            scale=Wend[:, pr, ci:ci + 1])
                                  else:
                                      P.ts("dve", N0[c][:, pr, :], bk2[:, pr * 128:(pr + 1) * 128], Wend[:, pr, ci:ci + 1], None,
                                           ALU.mult, None, [N0[c]], [bk2, Wend])
                          P.stop(11)
                          bky = nextbank()
                          bkr = nextbank()
                          fy = True
                          for pr in range(4):
                              yc = slice(pr * 128, (pr + 1) * 128)
                              for (lh, rh) in ((KU[pr][:, 128:256], AMh[pr][0][:, 384:512]), (KU[pr][:, 384:512], AMh[pr][1][:, 384:512]),
                                               (TM[pr][:, 0:128], AK[0][:, pr, :]), (TM[pr][:, 128:256], AK[1][:, pr, :])):
                                  P.mm(bky, bky[:, yc], lh, rh, fy, False, [KU[pr], AMh[pr][0], AMh[pr][1], AK[0], AK[1], TM[pr]],
                                       signal=False)
                                  fy = False
                          fr = True
                          for c in range(2):
                              ci = gi * 2 + c
                              zq, zp, znq, znp = ZQ[zcur], ZP[zcur], ZQ[1 - zcur], ZP[1 - zcur]
                              zqv, zpv, znqv, znpv = ZQv[zcur], ZPv[zcur], ZQv[1 - zcur], ZPv[1 - zcur]
                              for pr in range(4):
                                  cc_ = slice(pr * 128 + c * 64, pr * 128 + c * 64 + 64)
                                  P.mm(bky, bky[:, cc_], zq[:, pr, :], RT[:, pr, c * 64:(c + 1) * 64], False, True, [zq, zqv[pr], RT],
                                       signal=True)
                                  P.mm(bkr, bkr[:, cc_], zp[:, pr, :], RT[:, pr, c * 64:(c + 1) * 64], fr, True, [zp, zpv[pr], RT],
                                       signal=True)
                                  fr = False
                              bkq, bkp = nextbank(), nextbank()
                              for pr in range(4):
                                  P.mm(bkq, bkq[:, pr * 128:(pr + 1) * 128], MT[c][:, pr, :], zq[:, pr, :], pr == 0, True, [MT[c], zq, zqv[pr]],
                                       signal=(pr == 3))
                              for pr in range(4):
                                  P.mm(bkp, bkp[:, pr * 128:(pr + 1) * 128], MT[c][:, pr, :], zp[:, pr, :], pr == 0, True, [MT[c], zp, zpv[pr]],
                                       signal=(pr == 3))
                              for pr in range(4):
                                  dsc = Wend[:, pr, ci:ci + 1]
                                  P.stt(znq[:, pr, :], bkq[:, pr * 128:(pr + 1) * 128], dsc, N0[c][:, pr, :], ALU.mult, ALU.add,
                                        [znqv[pr]], [bkq, Wend, N0[c]])
                                  P.act(znp[:, pr, :], bkp[:, pr * 128:(pr + 1) * 128], AF.Copy, [znpv[pr]], [bkp, Wend], scale=dsc)
                              zcur = 1 - zcur
                          P.copy("act", yqblk[:, :, sl], bky[:].rearrange("p (a c) -> p a c", c=128), [yqblk], [bky])
                          P.copy("dve", rpblk[:, :, sl], bkr[:].rearrange("p (a c) -> p a c", c=128), [rpblk], [bkr])
                          ngroup += 1

                      if bi == 0:
                          P.stop(4)
                      for (dn, dt_, tl) in (("yq_s", yq_s, yqblk), ("rp_s", rp_s, rpblk), ("g_s", g_s, gblk), ("bon_s", bon_s, bonblk)):
                          P.dma("sp", sem_o, dt_.rearrange("a p t -> p a t")[:, :, t0:t0 + n], tl[:, :, 0:n], rd=[tl], dwr=[dn])
                except StopBuild:
                    pass
                S.wait_all("sp", [P.dbuf[nm] for nm in ("yq_s", "rp_s", "g_s", "bon_s", "cc_in")])
                if "yq" in P.debug:
                    pass
                S.emit()

        if "yq" in P.debug:
            with contextlib.ExitStack() as st:
                tl = P.sb(st, "dbgt", [128, 4, NT])
                o = P.dout("dbg_yq", [128, 4, NT])
                P.dma("sp", sem_o2, tl[:], yq_s.rearrange("a p t -> p a t"), wr=[tl], drd=["yq_s"])
                P.dma("sp", sem_o2, o, tl[:], rd=[tl], dwr=["dbg_yq"])
                S.wait_all("sp", [P.dbuf["dbg_yq"]])
                S.emit()

        st = contextlib.ExitStack()
        xb_stack = st
        if "X" in phases:
            if True:
                zall = P.sb(st, "zall", [128, 4, 1024])
                zs = P.sb(st, "zs", [128, 4, 128])
                zsb = P.sb(st, "zsb", [128, 4, 128], BF16)
                zpb = P.sb(st, "zpb", [128, 4, 128], BF16)
                zpt = P.sb(st, "zpt", [128, 4, 128], BF16)
                xt1 = P.sb(st, "xt1", [128, 4, 128])
                S.sems["cc"] = nc.alloc_semaphore(name="cc_sem")
                S.raw("pool", [], lambda e: e.collective_compute(
                    "AllGather", ALU.bypass, replica_groups=[[0, 1, 2, 3], [4, 5, 6, 7]],
                    ins=[P.dram["cc_in"].ap().opt()], outs=[P.dram["cc_out"].ap().opt()]), "cc", 1)
                P.dbuf["cc_out"].lw = ("cc", 1, "dma", 0)
                P.dma("pool", None, zall[:], cc_out.rearrange("(r p) f -> p r f", p=128), wr=[zall], drd=["cc_out"])
                def emit_fold():
                  P.memset("dve", zs[:], 0.0, [zs])
                  for c in range(3):
                      zv = zall[:, c, :].rearrange("p (a c) -> p a c", c=256)
                      P.copy("dve", zpb[:], zv[:, :, 128:256], [zpb], [zall])
                      P.copy("act", zsb[:], zs[:], [zsb], [zs])
                      bk = nextbank()
                      for pr in range(4):
                          P.mm(bk, bk[:, pr * 128:(pr + 1) * 128], zpb[:, pr, :], identb[:], pr == 0, True, [zpb, identb],
                               signal=(pr == 3))
                      P.copy("act", zpt[:].rearrange("p a c -> p (a c)"), bk[:], [zpt], [bk])
                      bk2 = nextbank()
                      for pr in range(4):
                          P.mm(bk2, bk2[:, pr * 128:(pr + 1) * 128], zpt[:, pr, :], zsb[:, pr, :], pr == 0, True, [zpt, zsb],
                               signal=(pr == 3))
                      P.tt("dve", xt1[:], bk2[:].rearrange("p (a c) -> p a c", c=128), zv[:, :, 0:128], ALU.add, [xt1], [bk2, zall])
                      P.tt("dve", xt1[:], xt1[:], zs[:], ALU.subtract, [xt1], [xt1, zs])
                      P.stt(zs[:], xt1[:], cmask[:, c:c + 1], zs[:], ALU.mult, ALU.add, [zs], [xt1, cmask, zs])
                  P.copy("dve", zst[:], zs[:], [zst], [zs])
                  P.dbg("zs", zs[:], [128, 4, 128], [zs])
                if "B" not in phases:
                    emit_fold()
                    S.emit()
                    xb_stack.close()

        if "B" in phases:
            if True:
                wcv = P.sb(st, "wcv", [128, 8, 1024], BF16)
                wout = P.sb(st, "wout", [128, 8, 1024], BF16)
                dgt = [[P.sb(st, "dgt%d_%d" % (i, e), [128, 16 - e, 128], BF16) for e in range(2)] for i in range(4)]
                cwv = P.sb(st, "cwv", [128, 4, 31])
                cvv = P.sb(st, "cvv", [128, 12])
                rvb = P.sb(st, "rvb", [128, 28])
                xb0 = P.sb(st, "xb2", [128, 8, 512])
                xb1 = T(zall.t[:].rearrange("p a (k c) -> p (a k) c", c=512), "xb_alias")
                xb1.b = zall.b
                xbs = [xb0, xb1]
                rstd = P.sb(st, "brstd", [128, 512])
                tmp = [P.sb(st, "btmp%d" % i, [128, 512]) for i in range(2)]
                hbs = [P.sb(st, "bhb%d" % i, [128, 8, 512], BF16) for i in range(2)]
                UGs = [P.sb(st, "UG%d" % i, [128, 4, 30 + 512], BF16) for i in range(2)]
                sigt = [P.sb(st, "sigt%d" % i, [128, 512], BF16) for i in range(2)]
                CB = P.sb(st, "CB", [128, 4, 512])
                cbb = [P.sb(st, "cbb%d" % i, [128, 512], BF16) for i in range(2)]
                csq = [P.sb(st, "csq%d" % i, [128, 512], BF16) for i in range(2)]
                lnm = P.sb(st, "lnm", [128, 512])
                lnq = P.sb(st, "lnq", [128, 512])
                lnr = P.sb(st, "lnr", [128, 512])
                UO = P.sb(st, "UO", [128, 4, 512], BF16)
                yqb = P.sb(st, "yqb", [128, 4, 512], BF16)
                rpb = P.sb(st, "rpb", [128, 4, 512], BF16)
                gb = P.sb(st, "gb", [128, 4, 512], BF16)
                bonb = P.sb(st, "bonb", [128, 4, 512], BF16)
                yv = [P.sb(st, "yv%d" % i, [128, 512]) for i in range(2)]
                ybf = [P.sb(st, "ybf%d" % i, [128, 512], BF16) for i in range(2)]
                ysq = [P.sb(st, "ysq%d" % i, [128, 512], BF16) for i in range(2)]
                sq = ysq
                ym, yq_, yr = lnm, lnq, lnr
                YO = P.sb(st, "YO", [128, 4, 512], BF16)
                msb = P.sb(st, "msb", [128, 8, 512])
                msbv = [T(msb.t[:, ot, :], "msb_ot%d" % ot) for ot in range(8)]

                load_x(xbs[0], blocks[0][0] + 1, blocks[0][1])
                load_x_and_norm(xbs[0], sigt, rstd, tmp, hbs[0], blocks[0][0] + 1, blocks[0][1], Am, 0, do_load=False)
                wst = [T(msb.t[:, 2 * i:2 * i + 2, :].rearrange("p a c -> p (a c)"), "wstB%d" % i) for i in range(4)]
                P.dma("sp", None, cwv[:], cw, wr=[cwv])
                P.dma("sp", None, cvv[:], cvecs, wr=[cvv])
                P.dma("sp", None, rvb[:], rwv, wr=[rvb])
                nst = 0
                for (dst_, srcw, c_lo) in ((wcv, w_in, 0), (wout, w_out, 0)):
                    for kt in range(8):
                        stg_ = wst[nst % 4]
                        P.dma("sp", None, stg_[:], srcw[kt * 128:(kt + 1) * 128, c_lo:c_lo + 1024], wr=[stg_])
                        P.copy("act" if nst % 2 else "dve", dst_[:, kt, :], stg_[:], [dst_], [stg_])
                        nst += 1
                    if dst_ is wcv:
                        for ct in range(4):
                            for k in range(31):
                                dg_ = dgt[ct][k % 2]
                                if k % 2 == 0:
                                    P.ts("dve", dg_[:, k // 2, :], identb[:], cwv[:, ct, k:k + 1], None, ALU.mult, None,
                                         [dg_], [identb, cwv])
                                else:
                                    P.act(dg_[:, k // 2, :], identb[:], AF.Copy, [dg_], [identb, cwv], scale=cwv[:, ct, k:k + 1])
                P.memset("dve", UGs[0][:], 0.0, [UGs[0]])
                P.memset("pool", UGs[1][:], 0.0, [UGs[1]])

                try:
                  P.stop(20)
                  def front_norm(bi):
                    t0, n = blocks[bi]
                    load_x_and_norm(xbs[bi % 2], sigt, rstd, tmp, hbs[bi % 2], t0 + 1, n, Am, 0, do_load=False)

                  def front(bi):
                    t0, n = blocks[bi]
                    xb, hb, UG = xbs[bi % 2], hbs[bi % 2], UGs[bi % 2]
                    if bi >= 1:
                        npv = blocks[bi - 1][1]
                        P.copy("pool", UG[:, :, 0:30], UGs[(bi - 1) % 2][:, :, npv:npv + 30], [UG], [UGs[(bi - 1) % 2]])
                    for ct in range(4):
                        bu, bg = nextbank(), nextbank()
                        for kt in range(8):
                            P.mm(bu, bu[:, 0:n], wcv[:, kt, ct * 128:(ct + 1) * 128], hb[:, kt, 0:n], kt == 0, kt == 7, [wcv, hb])
                        for kt in range(8):
                            P.mm(bg, bg[:, 0:n], wcv[:, kt, 512 + ct * 128:512 + (ct + 1) * 128], hb[:, kt, 0:n], kt == 0, kt == 7,
                                 [wcv, hb])
                        sg_ = sigt[ct % 2]
                        P.act(sg_[:, 0:n], bg[:, 0:n], AF.Sigmoid, [sg_], [bg])
                        if bi == 0:
                            P.stt(UG[:, ct, 30:30 + n], bu[:, 0:n], hv[:, 0:1], sg_[:, 0:n], ALU.mult, ALU.mult, [UG], [bu, hv, sg_])
                        else:
                            P.tt("dve", UG[:, ct, 30:30 + n], bu[:, 0:n], sg_[:, 0:n], ALU.mult, [UG], [bu, sg_])

                  def load_y(bj):
                    t0_, n_ = blocks[bj]
                    tsl = slice(t0_, t0_ + n_)
                    P.dma("sp", None, yqb[:, :, 0:n_], yq_s.rearrange("a p t -> p a t")[:, :, tsl], wr=[yqb], drd=["yq_s"])
                    P.dma("sp", None, rpb[:, :, 0:n_], rp_s.rearrange("a p t -> p a t")[:, :, tsl], wr=[rpb], drd=["rp_s"])
                    P.dma("sp", None, gb[:, :, 0:n_], g_s.rearrange("a p t -> p a t")[:, :, tsl], wr=[gb], drd=["g_s"])
                    P.dma("sp", None, bonb[:, :, 0:n_], bon_s.rearrange("a p t -> p a t")[:, :, tsl], wr=[bonb], drd=["bon_s"])

                  def back1(bi):
                    t0, n = blocks[bi]
                    UG = UGs[bi % 2]
                    if bi == 0:
                        emit_fold()
                        load_x(xbs[1], blocks[1][0] + 1, blocks[1][1])
                    if bi == 0:
                        load_y(0)
                    bs1, bs2 = nextbank(True), nextbank(True)
                    bsb[0], bsb[1] = bs1, bs2
                    def conv_ct(ct):
                        bc = nextbank()
                        for k in range(31):
                            P.mm(bc, bc[:, 0:n], dgt[ct][k % 2][:, k // 2, :], UG[:, ct, k:k + n], k == 0, k == 30, [dgt[ct][k % 2], UG])
                        P.act(CB[:, ct, 0:n], bc[:, 0:n], AF.Identity, [CB], [bc, cvv], bias=cvv[:, ct:ct + 1])
                        P.act(csq[ct % 2][:, 0:n], bc[:, 0:n], AF.Square, [csq[ct % 2]], [bc, cvv], bias=cvv[:, ct:ct + 1])
                        P.copy("pool", cbb[ct % 2][:, 0:n], CB[:, ct, 0:n], [cbb[ct % 2]], [CB])

                    def conv_stats(ct):
                        P.mm(bs1, bs1[:, 0:n], onesb[:], cbb[ct % 2][:, 0:n], ct == 0, ct == 3, [onesb, cbb[ct % 2]], signal=True)
                        P.mm(bs2, bs2[:, 0:n], onesb[:], csq[ct % 2][:, 0:n], ct == 0, ct == 3, [onesb, csq[ct % 2]], signal=True)

                    def y_a(pr):
                        by = nextbank()
                        y_ = yv[pr % 2]
                        P.mm(by, by[:, 0:n], zst[:, pr, :], rpb[:, pr, 0:n], True, True, [zst, rpb])
                        P.tt("dve", y_[:, 0:n], by[:, 0:n], yqb[:, pr, 0:n], ALU.add, [y_], [by, yqb])
                        P.act(ysq[pr % 2][:, 0:n], y_[:, 0:n], AF.Square, [ysq[pr % 2]], [y_])
                        P.copy("act", ybf[pr % 2][:, 0:n], y_[:, 0:n], [ybf[pr % 2]], [y_])

                    def y_b(pr):
                        b1, b2 = nextbank(), nextbank()
                        y_ = yv[pr % 2]
                        P.mm(b1, b1[:, 0:n], blkb[:], ybf[pr % 2][:, 0:n], True, True, [blkb, ybf[pr % 2]])
                        P.mm(b2, b2[:, 0:n], blkb[:], ysq[pr % 2][:, 0:n], True, True, [blkb, ysq[pr % 2]])
                        P.act(ym[:, 0:n], b1[:, 0:n], AF.Copy, [ym], [b1], scale=1.0 / 64)
                        P.tt("dve", yq_[:, 0:n], ym[:, 0:n], ym[:, 0:n], ALU.mult, [yq_], [ym])
                        P.stt(yq_[:, 0:n], b2[:, 0:n], 1.0 / 64, yq_[:, 0:n], ALU.mult, ALU.subtract, [yq_], [b2, yq_])
                        P.rsqrt(yr[:, 0:n], yq_[:, 0:n], LNX_EPS, 1.0, yr, [yq_])
                        P.tt("dve", y_[:, 0:n], y_[:, 0:n], ym[:, 0:n], ALU.subtract, [y_], [y_, ym])
                        P.tt("pool", y_[:, 0:n], y_[:, 0:n], yr[:, 0:n], ALU.mult, [y_], [y_, yr])
                        P.stt(y_[:, 0:n], y_[:, 0:n], rvb[:, 16 + pr:17 + pr], bonb[:, pr, 0:n], ALU.mult, ALU.add, [y_], [y_, rvb, bonb])
                        P.tt("dve", YO[:, pr, 0:n], y_[:, 0:n], gb[:, pr, 0:n], ALU.mult, [YO], [y_, gb])

                    for i_ in range(4):
                        conv_ct(i_)
                        if i_ >= 1:
                            conv_stats(i_ - 1)
                            y_b(i_ - 1)
                        y_a(i_)
                        if i_ == 1 and bi + 1 < len(blocks):
                            front_norm(bi + 1)
                    conv_stats(3)
                    y_b(3)

                  def back2(bi):
                    t0, n = blocks[bi]
                    xb = xbs[bi % 2]
                    bs1, bs2 = bsb
                    if bi + 1 < len(blocks):
                        load_y(bi + 1)
                    P.act(lnm[:, 0:n], bs1[:, 0:n], AF.Copy, [lnm], [bs1], scale=1.0 / 512)
                    P.tt("dve", lnq[:, 0:n], lnm[:, 0:n], lnm[:, 0:n], ALU.mult, [lnq], [lnm])
                    P.stt(lnq[:, 0:n], bs2[:, 0:n], 1.0 / 512, lnq[:, 0:n], ALU.mult, ALU.subtract, [lnq], [bs2, lnq])
                    release(bs1)
                    release(bs2)
                    P.rsqrt(lnr[:, 0:n], lnq[:, 0:n], LN_EPS, 1.0, lnr, [lnq])
                    for ct in range(4):
                        t_ = tmp[ct % 2]
                        P.tt("dve", t_[:, 0:n], CB[:, ct, 0:n], lnm[:, 0:n], ALU.subtract, [t_], [CB, lnm])
                        P.tt("pool", t_[:, 0:n], t_[:, 0:n], lnr[:, 0:n], ALU.mult, [t_], [t_, lnr])
                        P.act(UO[:, ct, 0:n], t_[:, 0:n], AF.Silu, [UO], [t_, cvv], bias=cvv[:, 8 + ct:9 + ct], scale=cvv[:, 4 + ct:5 + ct])
                    if bi == 1:
                        P.dbg("uo", UO[:, 0, 0:n], [128, 512], [UO], BF16)
                    if bi == 1:
                        P.dbg("yo", YO[:, 0, 0:n], [128, 512], [YO], BF16)
                    P.stop(25)
                    bss = nextbank(True)
                    for ot in range(8):
                        bm = nextbank()
                        for kt in range(4):
                            P.mm(bm, bm[:, 0:n], wout[:, kt, ot * 128:(ot + 1) * 128], UO[:, kt, 0:n], kt == 0, False, [wout, UO])
                        for kt in range(4):
                            P.mm(bm, bm[:, 0:n], wout[:, 4 + kt, ot * 128:(ot + 1) * 128], YO[:, kt, 0:n], False, kt == 3, [wout, YO])
                        P.act(sq[ot % 2][:, 0:n], bm[:, 0:n], AF.Square, [sq[ot % 2]], [bm])
                        P.copy("dve", msb[:, ot, 0:n], bm[:, 0:n], [msbv[ot]], [bm])
                        if ot >= 1:
                            P.mm(bss, bss[:, 0:n], onesb[:], sq[(ot - 1) % 2][:, 0:n], ot == 1, False, [onesb, sq[(ot - 1) % 2]], signal=True)
                    P.mm(bss, bss[:, 0:n], onesb[:], sq[7 % 2][:, 0:n], False, True, [onesb, sq[7 % 2]], signal=True)
                    release(bss)
                    P.stop(26)
                    P.rsqrt(rstd[:, 0:n], bss[:, 0:n], float(D * RMS_EPS), 1.0, rstd, [bss])
                    for ot in range(8):
                        P.tt("pool" if ot % 2 else "dve", msb[:, ot, 0:n], msb[:, ot, 0:n], rstd[:, 0:n], ALU.mult, [msbv[ot]], [msbv[ot], rstd])
                        P.stt(msb[:, ot, 0:n], msb[:, ot, 0:n], Gm[:, ot:ot + 1], xb[:, ot, 0:n], ALU.mult, ALU.add, [msbv[ot]], [msbv[ot], Gm, xb])
                    xmv = xm_s.rearrange("(kt p) t -> p kt t", p=128)
                    if bi == 0:
                        P.dma("sp", None, xmv[:, :, 0:2], msb[:, :, n - 2:n], rd=[msb] + msbv, dwr=["xm_s"])
                    else:
                        P.dma("sp", None, xmv[:, :, 2 + t0 - HALO:2 + t0 - HALO + n], msb[:, :, 0:n], rd=[msb] + msbv, dwr=["xm_s"])
                    if bi + 2 < len(blocks):
                        load_x(xbs[bi % 2], blocks[bi + 2][0] + 1, blocks[bi + 2][1])

                  bsb = [None, None]
                  front(0)
                  for bi in range(len(blocks)):
                      back1(bi)
                      if bi + 1 < len(blocks):
                          front(bi + 1)
                      back2(bi)
                except StopBuild:
                    for b_ in list(held):
                        held.discard(b_)
                S.wait_all("sp", [P.dbuf["xm_s"]])
                S.emit()
                xb_stack.close()

        if "xm" in P.debug:
            with contextlib.ExitStack() as st:
                tl = P.sb(st, "dbgx", [128, 8, 514])
                o = P.dout("dbg_xm", [128, 8, 514])
                P.dma("sp", None, tl[:], xm_s.rearrange("(kt p) t -> p kt t", p=128)[:, :, 0:514], wr=[tl], drd=["xm_s"])
                P.dma("sp", None, o, tl[:], rd=[tl], dwr=["dbg_xm"])
                S.wait_all("sp", [P.dbuf["dbg_xm"]])
                S.emit()

        if "C" in phases:
            with contextlib.ExitStack() as st:
                wdn = P.sb(st, "wdn", [128, 22, 1024], BF16)
                wup = [[P.sb(st, "wup%d_%d" % (i, j), [128, 8, 128], BF16) for j in range(2)] for i in range(2)]
                fwv = P.sb(st, "fwv", [128, 44, 3])
                fbv = P.sb(st, "fbv", [128, 44])
                xma = P.sb(st, "xma", [128, 8, 514])
                xmb = P.sb(st, "xmb", [128, 8, 512])
                h2 = P.sb(st, "h2", [128, 8, 1026], BF16)
                Gt = P.sb(st, "Gt", [128, 22, 1024], BF16)
                ZS = [[P.sb(st, "ZS%d_%d" % (i, j), [128, 514], BF16) for j in range(2)] for i in range(2)]
                dgc = [[P.sb(st, "dgc%d_%d" % (i, j), [128, 3, 128], BF16) for j in range(2)] for i in range(2)]
                acc = [[P.sb(st, "acc%d_%d" % (i, j), [128, 512]) for j in range(2)] for i in range(2)]
                fsb = P.sb(st, "fsb", [128, 8, 512])
                fsbv = [T(fsb.t[:, ot, :], "fsb_ot%d" % ot) for ot in range(8)]
                sq = [P.sb(st, "csq2_%d" % i, [128, 512], BF16) for i in range(2)]
                rstd = acc[1][0]
                tmp = [acc[0][0], acc[0][1]]

                wst = [P.sb(st, "wstC%d" % i, [128, 1024]) for i in range(6)]
                xmv = xm_s.rearrange("(kt p) t -> p kt t", p=128)

                def load_xm(half, part):
                    if part == 0:
                        P.dma("sp", None, xma[:], xmv[:, :, half * 1024:half * 1024 + 514], wr=[xma], drd=["xm_s"])
                    else:
                        P.dma("sp", None, xmb[:], xmv[:, :, half * 1024 + 514:half * 1024 + 1026], wr=[xmb], drd=["xm_s"])

                load_xm(0, 0)
                load_xm(0, 1)
                P.dma("sp", None, fwv[:], fw, wr=[fwv])
                P.dma("sp", None, fbv[:], fb, wr=[fbv])
                wdst = [wst[4], wst[5]]
                wupv = w_up.rearrange("(kt p) n -> p kt n", p=128)
                xmv = xm_s.rearrange("(kt p) t -> p kt t", p=128)

                def norm_cols(xm, sc0, c0, ncol):
                    bk = nextbank()
                    for kt in range(8):
                        s_ = sq[kt % 2]
                        P.act(s_[:, 0:ncol], xm[:, kt, sc0:sc0 + ncol], AF.Square, [s_], [xm])
                        P.mm(bk, bk[:, 0:ncol], onesb[:], s_[:, 0:ncol], kt == 0, kt == 7, [onesb, s_], signal=True)
                    P.rsqrt(rstd[:, 0:ncol], bk[:, 0:ncol], float(D * RMS_EPS), 1.0, rstd, [bk])
                    for kt in range(8):
                        t_ = tmp[kt % 2]
                        P.tt("dve", t_[:, 0:ncol], xm[:, kt, sc0:sc0 + ncol], rstd[:, 0:ncol], ALU.mult, [t_], [xm, rstd])
                        P.act(h2[:, kt, c0:c0 + ncol], t_[:, 0:ncol], AF.Identity, [h2], [t_, Af, modv],
                              bias=modv[:, 24 + kt:25 + kt], scale=Af[:, kt:kt + 1])

                nw = 0
                for half in range(2):
                    if half == 0:
                        norm_cols(xma, 0, 0, 2)
                        norm_cols(xma, 2, 2, 512)
                        norm_cols(xmb, 0, 514, 512)
                    if half == 0:
                        P.ts("dve", h2[:, :, 0:2], h2[:, :, 0:2], hv[:, 0:1], None, ALU.mult, None, [h2], [h2, hv])
                    def w_dma(z_):
                        sg_, sv_ = wst[(z_ % 2) * 2], wst[(z_ % 2) * 2 + 1]
                        P.dma("sp", None, sg_[:].rearrange("p (k c) -> p k c", c=128), wupv[:, :, z_ * 128:(z_ + 1) * 128], wr=[sg_])
                        P.dma("sp", None, sv_[:].rearrange("p (k c) -> p k c", c=128), wupv[:, :, DFF + z_ * 128:DFF + (z_ + 1) * 128],
                              wr=[sv_])

                    def w_cast(z_):
                        sg_, sv_ = wst[(z_ % 2) * 2], wst[(z_ % 2) * 2 + 1]
                        wg_, wv_ = wup[z_ % 2]
                        P.copy("dve", wg_[:].rearrange("p k c -> p (k c)"), sg_[:], [wg_], [sg_])
                        P.copy("dve", wv_[:].rearrange("p k c -> p (k c)"), sv_[:], [wv_], [sv_])

                    w_dma(0)
                    w_dma(1)
                    w_cast(0)
                    pend = []
                    for zt in range(22):
                        wg, wv = wup[zt % 2]
                        if zt + 2 < 22:
                            w_dma(zt + 2)
                        if zt + 1 < 22:
                            w_cast(zt + 1)
                        if half == 0:
                            P.dma("sp", None, wdst[zt % 2][:], w_down[zt * 128:(zt + 1) * 128, :], wr=[wdst[zt % 2]])
                            P.copy("act", wdn[:, zt, :], wdst[zt % 2][:], [wdn], [wdst[zt % 2]])
                        zsg, zsv = ZS[zt % 2]
                        for j_, zi in ((0, zt), (1, zt + 22)):
                            for k in range(3):
                                P.ts("dve", dgc[zt % 2][j_][:, k, :], identb[:], fwv[:, zi, k:k + 1], None, ALU.mult, None,
                                     [dgc[zt % 2][j_]], [identb, fwv])
                        bh = nextbank()
                        for kt in range(8):
                            P.mm(bh, bh[:, 0:2], wg[:, kt, :], h2[:, kt, 0:2], kt == 0, kt == 7, [wg, h2])
                        for kt in range(8):
                            P.mm(bh, bh[:, 2:4], wv[:, kt, :], h2[:, kt, 0:2], False, kt == 7, [wv, h2])
                        P.copy("act", zsg[:, 0:2], bh[:, 0:2], [zsg], [bh])
                        P.copy("act", zsv[:, 0:2], bh[:, 2:4], [zsv], [bh])
                        for sb_ in range(2):
                            for (w_, zs_, j_) in ((wg, zsg, 0), (wv, zsv, 1)):
                                c0 = 2 + sb_ * 512
                                bz = nextbank()
                                for kt in range(8):
                                    P.mm(bz, bz[:], w_[:, kt, :], h2[:, kt, c0:c0 + 512], kt == 0, kt == 7, [w_, h2])
                                P.copy("act", zs_[:, 2:514], bz[:], [zs_], [bz])
                                if pend:
                                    pend.pop(0)()

                                def stage2(zt=zt, sb_=sb_, j_=j_, zs_=zs_):
                                    bc = nextbank()
                                    dg_ = dgc[zt % 2][j_]
                                    for k in range(3):
                                        P.mm(bc, bc[:], dg_[:, k, :], zs_[:, k:k + 512], k == 0, k == 2, [dg_, zs_])
                                    P.copy("pool", zs_[:, 0:2], zs_[:, 512:514], [zs_], [zs_])
                                    ag = acc[sb_][0]
                                    if j_ == 0:
                                        P.act(ag[:], bc[:], AF.Silu, [ag], [bc, fbv], bias=fbv[:, zt:zt + 1])
                                    else:
                                        P.stt(Gt[:, zt, sb_ * 512:(sb_ + 1) * 512], bc[:], fbv[:, zt + 22:zt + 23], ag[:], ALU.add, ALU.mult,
                                              [Gt], [bc, fbv, ag])
                                pend.append(stage2)
                    while pend:
                        pend.pop(0)()
                    if half == 0:
                        P.dbg("G", Gt[:, 0, 0:512], [128, 512], [Gt], BF16)
                    for sb_ in range(2):
                        bss = nextbank(True)
                        for ot in range(8):
                            bm = nextbank()
                            for zt in range(22):
                                P.mm(bm, bm[:], wdn[:, zt, ot * 128:(ot + 1) * 128], Gt[:, zt, sb_ * 512:(sb_ + 1) * 512],
                                     zt == 0, zt == 21, [wdn, Gt])
                            P.act(sq[ot % 2][:], bm[:], AF.Square, [sq[ot % 2]], [bm])
                            P.copy("dve", fsb[:, ot, :], bm[:], [fsbv[ot]], [bm])
                            if ot >= 1:
                                P.mm(bss, bss[:], onesb[:], sq[(ot - 1) % 2][:], ot == 1, False, [onesb, sq[(ot - 1) % 2]], signal=True)
                        P.mm(bss, bss[:], onesb[:], sq[7 % 2][:], False, True, [onesb, sq[7 % 2]], signal=True)
                        release(bss)
                        if half == 0 and sb_ == 1:
                            norm_cols(xma, 0, 0, 2)
                            norm_cols(xma, 2, 2, 512)
                        P.rsqrt(rstd[:], bss[:], float(D * RMS_EPS), 1.0, rstd, [bss])
                        xsrc = xma[:, :, 2:514] if sb_ == 0 else xmb[:, :, 0:512]
                        xt_ = xma if sb_ == 0 else xmb
                        for ot in range(8):
                            P.tt("pool" if ot % 2 else "dve", fsb[:, ot, :], fsb[:, ot, :], rstd[:], ALU.mult, [fsbv[ot]], [fsbv[ot], rstd])
                            P.stt(fsb[:, ot, :], fsb[:, ot, :], Gf[:, ot:ot + 1], xsrc[:, ot, :],
                                  ALU.mult, ALU.add, [fsbv[ot]], [fsbv[ot], Gf, xt_])
                        oc = half * 1024 + sb_ * 512
                        P.dma("sp", None, outT.rearrange("(kt p) t -> p kt t", p=128)[:, :, oc:oc + 512], fsb[:], rd=[fsb] + fsbv, dwr=["outT"])
                        if half == 0:
                            load_xm(1, sb_)
                            if sb_ == 1:
                                norm_cols(xmb, 0, 514, 512)
                S.wait_all("sp", [P.dbuf["outT"]])
                S.emit()

        P.finish_outputs = []
        if "C" not in phases:
            with contextlib.ExitStack() as st:
                zt = P.sb(st, "zt", [128, TOK])
                P.memset("dve", zt[:], 0.0, [zt])
                for kt in range(8):
                    P.dma("sp", sem_o2, outT[kt * 128:(kt + 1) * 128, :], zt[:], rd=[zt], dwr=["outT"])
                S.wait_all("sp", [P.dbuf["outT"]])
                S.emit()
    return P


def _tiles(v, nt):
    return np.ascontiguousarray(np.asarray(v, np.float32).reshape(nt, 128).T)


def _consts():
    c = np.zeros((128, 5, 128), np.float32)
    idx = np.arange(128)
    same = (idx[:, None] // 64) == (idx[None, :] // 64)
    c[:, 0, :] = np.eye(128)
    c[:, 1, :] = same & (idx[None, :] < idx[:, None])
    c[:, 2, :] = same & (idx[None, :] > idx[:, None])
    c[:, 3, :] = same & (idx[None, :] >= idx[:, None])
    c[:, 4, :] = same
    return c


def prep_inputs(inp):
    f = lambda n: np.asarray(inp[n], np.float32)
    x = f("x")
    shared = {
        "gvec": np.concatenate([_tiles(f(n)[0], 8) for n in ("mix_pre_g", "mix_post_g", "ffn_pre_g", "ffn_post_g")], 1),
        "w_in": np.ascontiguousarray(f("w_in")[0]),
        "conv_w": np.ascontiguousarray(f("conv_dw_w")[0].T.reshape(4, 128, 31).transpose(1, 0, 2)),
        "conv_v": np.concatenate([_tiles(f(n)[0], 4) for n in ("conv_dw_b", "conv_ln_g", "conv_ln_b")], 1),
        "mu": _tiles(np.concatenate([f("rwkv_mu")[0], np.zeros(15 * 128 - NRW, np.float32)]), 15),
        "rwv": np.concatenate([_tiles(f(n)[0].reshape(-1), 4) for n in ("w0", "a0", "k_k", "k_a", "lnx_g", "lnx_b", "r_k")], 1),
        "w2a2": np.ascontiguousarray(np.concatenate([f("w2")[0], f("a2")[0]], 0)),
        "g2": np.ascontiguousarray(f("g2")[0]),
        "w_out": np.ascontiguousarray(f("w_out")[0]),
        "w_up": np.ascontiguousarray(f("w_up")[0]),
        "ffn_w": np.ascontiguousarray(f("ffn_dw_w")[0].T.reshape(44, 128, 3).transpose(1, 0, 2)),
        "ffn_b": _tiles(f("ffn_dw_b")[0], 44),
        "w_down": np.ascontiguousarray(f("w_down")[0]),
        "consts": _consts(),
    }
    maps = []
    for core in range(NCORE):
        b, q = divmod(core, 4)
        xt = np.zeros((D, NX), np.float32)
        lo = q * TOK - (HALO + 1)
        if q == 0:
            xt[:, HALO + 1:] = x[b, 0:TOK].T
        else:
            xt[:, :] = x[b, lo:lo + NX].T
        m = dict(shared)
        m["xT"] = xt
        m["ada_w"] = np.ascontiguousarray(f("ada_w")[0][:, q * 1536:(q + 1) * 1536])
        m["ada_b"] = _tiles(f("ada_b")[0][q * 1536:(q + 1) * 1536], 12)
        m["cvec"] = _tiles(f("c")[b], 8)
        m["hv"] = np.full((128, 1), 0.0 if q == 0 else 1.0, np.float32)
        cm = np.zeros((128, 4), np.float32)
        cm[:, :q] = 1.0
        m["cmask"] = cm
        maps.append(m)
    return maps


_CACHE = {}


def kernel(**inputs):
    if "prog" not in _CACHE:
        _CACHE["prog"] = build_program()
    P = _CACHE["prog"]
    maps = prep_inputs(inputs)
    res = run_bass_kernel_spmd(P.nc, maps, core_ids=list(range(NCORE)))
    out = np.zeros((2, SEQ, D), np.float32)
    for core in range(NCORE):
        b, q = divmod(core, 4)
        out[b, q * TOK:(q + 1) * TOK, :] = np.asarray(res.results[core]["outT"]).T
    return out
```
